# Optimizing a Trainium2 kernel written in Bass

```python
import math
import jax, jax.numpy as jnp
from jax import lax
import numpy as np

D_MODEL = 1024
BATCH = 4
SEQ = 8192
DEPTH = 4

N_MIXERS = 3
EPS = 1e-6
D_FF = 2816
CONV_WIDTH = 3
MLA_HEADS = 16
Q_LORA = 256
KV_LORA = 128
QK_NOPE = 64
QK_ROPE = 32
QK_HEAD = QK_NOPE + QK_ROPE
V_HEAD = 64
ROPE_THETA = 10000.0
Q_BLOCK = 128
SG_WIDTH = D_MODEL
SG_GROUPS = 8
SG_CHUNK = 128
N_A = (DEPTH + 2) // 3
N_B = (DEPTH + 1) // 3
N_C = DEPTH // 3

kernel_name = "hybrid_interleaved_macaron_trunk"


def rms_norm(x, g):
    xf = x.astype(jnp.float32)
    y = xf * lax.rsqrt(jnp.mean(xf * xf, axis=-1, keepdims=True) + EPS)
    return y.astype(x.dtype) * g


def swiglu(x, w_gate, w_up, w_down):
    return (jax.nn.silu(x @ w_gate) * (x @ w_up)) @ w_down


def rope(t, cos, sin):
    t1, t2 = jnp.split(t, 2, axis=-1)
    return jnp.concatenate([t1 * cos - t2 * sin, t2 * cos + t1 * sin], axis=-1)


def short_conv_mixer(x, w_in, conv_k, w_out):
    d = x.shape[-1]
    proj = x @ w_in
    b_gate, c_gate, h = proj[..., :d], proj[..., d:2 * d], proj[..., 2 * d:]
    z = c_gate * h
    conv = lax.conv_general_dilated(
        z, conv_k[:, None, :].astype(z.dtype), window_strides=(1,),
        padding=[(CONV_WIDTH - 1, 0)], dimension_numbers=("NWC", "WIO", "NWC"),
        feature_group_count=d)
    return (b_gate * conv) @ w_out


def mla_mixer(x, cos, sin, w_a, q_norm, w_uq, kv_norm, w_ukv, q_gain, k_gain, w_o):
    bsz, s, _ = x.shape
    a = x @ w_a
    c_q = a[..., :Q_LORA]
    c_kv = a[..., Q_LORA:Q_LORA + KV_LORA]
    k_pe = a[..., Q_LORA + KV_LORA:]
    q = (rms_norm(c_q, q_norm) @ w_uq).reshape(bsz, s, MLA_HEADS, QK_HEAD)
    kv = (rms_norm(c_kv, kv_norm) @ w_ukv).reshape(bsz, s, MLA_HEADS, QK_NOPE + V_HEAD)
    k_nope, v = kv[..., :QK_NOPE], kv[..., QK_NOPE:]
    k = jnp.concatenate(
        [k_nope, jnp.broadcast_to(k_pe[:, :, None, :], (bsz, s, MLA_HEADS, QK_ROPE))], axis=-1)
    q = rms_norm(q, q_gain)
    k = rms_norm(k, k_gain)
    cos_h, sin_h = cos[:, :, None, :], sin[:, :, None, :]
    q = jnp.concatenate([q[..., :QK_NOPE], rope(q[..., QK_NOPE:], cos_h, sin_h)], axis=-1)
    k = jnp.concatenate([k[..., :QK_NOPE], rope(k[..., QK_NOPE:], cos_h, sin_h)], axis=-1)
    q = q.transpose(0, 2, 1, 3)
    k = k.transpose(0, 2, 1, 3)
    v = v.transpose(0, 2, 1, 3)
    scale = QK_HEAD ** -0.5
    outs = []
    for blk in range(s // Q_BLOCK):
        q0, end = blk * Q_BLOCK, (blk + 1) * Q_BLOCK
        qb = q[:, :, q0:end]
        kb, vb = k[:, :, :end], v[:, :, :end]
        sc = jnp.einsum("bhqd,bhkd->bhqk", qb, kb).astype(jnp.float32) * scale
        mask = jnp.arange(end)[None, :] <= jnp.arange(q0, end)[:, None]
        sc = jnp.where(mask[None, None], sc, -jnp.inf)
        p = jax.nn.softmax(sc, axis=-1).astype(vb.dtype)
        outs.append(jnp.einsum("bhqk,bhkd->bhqd", p, vb))
    o = jnp.concatenate(outs, axis=2)
    o = o.transpose(0, 2, 1, 3).reshape(bsz, s, MLA_HEADS * V_HEAD)
    return o @ w_o


def spatial_gating_mixer(x, w_in, v_norm, w_s, b_s, w_out):
    bsz, s, _ = x.shape
    z = jax.nn.gelu(x @ w_in)
    u, v = z[..., :SG_WIDTH], z[..., SG_WIDTH:]
    v = rms_norm(v, v_norm)
    cg = SG_WIDTH // SG_GROUPS
    v = v.reshape(bsz, s // SG_CHUNK, SG_CHUNK, SG_GROUPS, cg)
    causal = jnp.tril(jnp.ones((SG_CHUNK, SG_CHUNK), dtype=bool))
    w_masked = jnp.where(causal[None], w_s, jnp.zeros((), w_s.dtype))
    mixed = jnp.einsum("gts,bcsgk->bctgk", w_masked, v) + b_s.T[None, None, :, :, None]
    return (u * mixed.reshape(bsz, s, SG_WIDTH)) @ w_out


def setup_inputs(seed: int = 0) -> dict:
    key = jax.random.key(seed)
    ks = jax.random.split(key, 24)

    def nrm(k, shape, fan_in):
        return jax.random.normal(k, shape, jnp.float32) * fan_in ** -0.5

    def gain(k, shape):
        return 1.0 + 0.05 * jax.random.normal(k, shape, jnp.float32)

    x = jax.random.normal(ks[0], (BATCH, SEQ, D_MODEL), jnp.float32)
    start = jax.random.randint(ks[1], (BATCH,), 0, 4096, dtype=jnp.int32)
    positions = (start[:, None] + jnp.arange(SEQ, dtype=jnp.int32)[None, :]).astype(jnp.int32)
    return {
        "x": x,
        "positions": positions,
        "norm_g": gain(ks[2], (DEPTH, 3, D_MODEL)),
        "ffn_gate": nrm(ks[3], (DEPTH, 2, D_MODEL, D_FF), D_MODEL),
        "ffn_up": nrm(ks[4], (DEPTH, 2, D_MODEL, D_FF), D_MODEL),
        "ffn_down": nrm(ks[5], (DEPTH, 2, D_FF, D_MODEL), D_FF),
        "conv_w_in": nrm(ks[6], (N_A, D_MODEL, 3 * D_MODEL), D_MODEL),
        "conv_k": nrm(ks[7], (N_A, CONV_WIDTH, D_MODEL), CONV_WIDTH),
        "conv_w_out": nrm(ks[8], (N_A, D_MODEL, D_MODEL), D_MODEL),
        "mla_w_a": nrm(ks[9], (N_B, D_MODEL, Q_LORA + KV_LORA + QK_ROPE), D_MODEL),
        "mla_q_norm": gain(ks[10], (N_B, Q_LORA)),
        "mla_w_uq": nrm(ks[11], (N_B, Q_LORA, MLA_HEADS * QK_HEAD), Q_LORA),
        "mla_kv_norm": gain(ks[12], (N_B, KV_LORA)),
        "mla_w_ukv": nrm(ks[13], (N_B, KV_LORA, MLA_HEADS * (QK_NOPE + V_HEAD)), KV_LORA),
        "mla_q_gain": gain(ks[14], (N_B, QK_HEAD)),
        "mla_k_gain": gain(ks[15], (N_B, QK_HEAD)),
        "mla_w_o": nrm(ks[16], (N_B, MLA_HEADS * V_HEAD, D_MODEL), MLA_HEADS * V_HEAD),
        "sg_w_in": nrm(ks[17], (N_C, D_MODEL, 2 * SG_WIDTH), D_MODEL),
        "sg_v_norm": gain(ks[18], (N_C, SG_WIDTH)),
        "sg_w_s": nrm(ks[19], (N_C, SG_GROUPS, SG_CHUNK, SG_CHUNK), SG_CHUNK),
        "sg_b": 1.0 + 0.1 * jax.random.normal(ks[20], (N_C, SG_GROUPS, SG_CHUNK), jnp.float32),
        "sg_w_out": nrm(ks[21], (N_C, SG_WIDTH, D_MODEL), SG_WIDTH),
    }


def reference(x, positions, norm_g, ffn_gate, ffn_up, ffn_down,
              conv_w_in, conv_k, conv_w_out,
              mla_w_a, mla_q_norm, mla_w_uq, mla_kv_norm, mla_w_ukv, mla_q_gain, mla_k_gain, mla_w_o,
              sg_w_in, sg_v_norm, sg_w_s, sg_b, sg_w_out):
    inv_freq = 1.0 / (ROPE_THETA ** (jnp.arange(0, QK_ROPE, 2, dtype=jnp.float32) / QK_ROPE))
    ang = positions.astype(jnp.float32)[..., None] * inv_freq
    cos = jnp.cos(ang).astype(x.dtype)
    sin = jnp.sin(ang).astype(x.dtype)

    ia = ib = ic = 0
    for i in range(DEPTH):
        x = x + 0.5 * swiglu(rms_norm(x, norm_g[i, 0]), ffn_gate[i, 0], ffn_up[i, 0], ffn_down[i, 0])
        hn = rms_norm(x, norm_g[i, 1])
        kind = i % N_MIXERS
        if kind == 0:
            mix = short_conv_mixer(hn, conv_w_in[ia], conv_k[ia], conv_w_out[ia])
            ia += 1
        elif kind == 1:
            mix = mla_mixer(hn, cos, sin, mla_w_a[ib], mla_q_norm[ib], mla_w_uq[ib], mla_kv_norm[ib],
                            mla_w_ukv[ib], mla_q_gain[ib], mla_k_gain[ib], mla_w_o[ib])
            ib += 1
        else:
            mix = spatial_gating_mixer(hn, sg_w_in[ic], sg_v_norm[ic], sg_w_s[ic], sg_b[ic], sg_w_out[ic])
            ic += 1
        x = x + mix
        x = x + 0.5 * swiglu(rms_norm(x, norm_g[i, 2]), ffn_gate[i, 1], ffn_up[i, 1], ffn_down[i, 1])
    return x
```

```python
import numpy as np
import concourse.bass as bass
import concourse.mybir as mybir
from concourse.bass_utils import run_bass_kernel_spmd

F32 = mybir.dt.float32
BF16 = mybir.dt.bfloat16
I32 = mybir.dt.int32
AF = mybir.ActivationFunctionType
ALU = mybir.AluOpType

D = 1024
DFF = 2816
KC = 8
FC = 22
T = 512
EPS = 1e-6
NCORES = 8
SLOT = 2048


class Res:
    __slots__ = ("w", "r", "name")

    def __init__(self, name=""):
        self.w = None
        self.r = []
        self.name = name


class Op:
    __slots__ = ("eng", "fn", "waits", "sig", "val", "key", "seq", "inc")


class Sched:
    COMPUTE = ("pe", "act", "dve", "pool", "sp")

    def __init__(self, nc, n_dma_sems=12, same_engine_sync=True):
        self.nc = nc
        self.engobj = {"pe": nc.tensor, "act": nc.scalar, "dve": nc.vector,
                       "pool": nc.gpsimd, "sp": nc.sync}
        self.ops = []
        self.sems = {}
        self.seqc = {}
        self.seen = {e: {} for e in self.COMPUTE}
        self.same = same_engine_sync
        self.nds = n_dma_sems
        self.dnext = {"sp": 0, "pool": 0, "act": 0}
        self.dlast = {}
        self.lastop = {}
        for e in self.COMPUTE:
            self._key(e)

    def _key(self, key):
        if key not in self.sems:
            self.sems[key] = self.nc.alloc_semaphore("s_" + str(key).replace(" ", ""))
            self.seqc[key] = 0
        return key

    def _record(self, eng, fn, reads, writes, key, inc, extra_deps=()):
        op = Op()
        op.eng = eng
        op.fn = fn
        op.key = key
        op.inc = inc
        op.sig = (inc == 16) or key == "cc"
        op.val = None
        self.seqc[key] += 1
        op.seq = self.seqc[key]
        deps = set(extra_deps)
        for r in reads:
            if r.w is not None:
                deps.add(r.w)
        for w in writes:
            if w.w is not None:
                deps.add(w.w)
            deps.update(w.r)
        waits = {}
        for d in deps:
            if d.key == eng:
                if eng in ("pe", "sp") or not self.same:
                    continue
            if self.seen[eng].get(d.key, 0) >= d.seq:
                continue
            if d.key not in waits or waits[d.key].seq < d.seq:
                waits[d.key] = d
        for k, d in waits.items():
            self.seen[eng][k] = d.seq
            d.sig = True
        op.waits = list(waits.values())
        for r in reads:
            r.r.append(op)
        for w in writes:
            w.w = op
            w.r = []
        self.ops.append(op)
        if fn is not None:
            self.lastop[key] = op
        return op

    def barrier(self):
        lasts = list(self.lastop.values())
        for eng in self.COMPUTE:
            self._record(eng, None, [], [], eng, 1, extra_deps=lasts)

    def op(self, eng, fn, reads=(), writes=()):
        return self._record(eng, fn, reads, writes, eng, 1)

    def mm(self, out, lhsT, rhs, start, stop, reads, writes, **kw):
        return self.op("pe", lambda e: e.matmul(out=out, lhsT=lhsT, rhs=rhs, start=start, stop=stop, **kw), reads, writes)

    def act(self, out, in_, func, reads, writes, eng="act", **kw):
        return self.op(eng, lambda e: e.activation(out=out, in_=in_, func=func, **kw), reads, writes)

    def tt(self, out, in0, in1, op, reads, writes, eng="dve"):
        return self.op(eng, lambda e: e.tensor_tensor(out=out, in0=in0, in1=in1, op=op), reads, writes)

    def stt(self, out, in0, scalar, in1, op0, op1, reads, writes, eng="dve"):
        return self.op(eng, lambda e: e.scalar_tensor_tensor(out=out, in0=in0, scalar=scalar, in1=in1, op0=op0, op1=op1), reads, writes)

    def ts(self, out, in0, scalar1, scalar2, op0, op1, reads, writes, eng="dve"):
        return self.op(eng, lambda e: e.tensor_scalar(out=out, in0=in0, scalar1=scalar1, scalar2=scalar2, op0=op0, op1=op1), reads, writes)

    def copy(self, out, in_, reads, writes, eng="dve"):
        return self.op(eng, lambda e: e.tensor_copy(out=out, in_=in_), reads, writes)

    def memset(self, ap, val, writes, eng="dve"):
        return self.op(eng, lambda e: e.memset(ap, val), (), writes)

    def dma(self, out, in_, reads=(), writes=(), eng="sp"):
        k = self.dnext[eng]
        self.dnext[eng] = (k + 1) % self.nds
        key = self._key(("d", eng, k))
        extra = []
        if key in self.dlast:
            extra.append(self.dlast[key])
        op = self._record(eng, lambda e: e.dma_start(out=out, in_=in_), reads, writes, key, 16, extra)
        self.dlast[key] = op
        return op

    def collective(self, fn, reads=(), writes=()):
        key = self._key("cc")
        return self._record("pool", fn, reads, writes, key, 1)

    def finish_wait(self, eng, ops):
        self._record(eng, None, [], [], eng, 1, extra_deps=ops)

    def emit(self):
        cnt = {k: 0 for k in self.sems}
        for op in self.ops:
            e = self.engobj[op.eng]
            for d in op.waits:
                e.wait_ge(self.sems[d.key], d.val)
            if op.fn is None:
                continue
            inst = op.fn(e)
            if op.sig:
                cnt[op.key] += op.inc
                op.val = cnt[op.key]
                inst.then_inc(self.sems[op.key], op.inc)


class Ring:
    def __init__(self, S, nc, nslots, sb, name="ring"):
        self.S = S
        self.t = [sb(f"{name}{i}", [128, SLOT], BF16) for i in range(nslots)]
        self.r = [Res(f"{name}{i}") for i in range(nslots)]
        self.i = 0

    def load(self, src_ap, src_res, n=SLOT):
        i = self.i
        self.i = (i + 1) % len(self.t)
        self.S.dma(self.t[i][:, 0:n], src_ap, reads=[src_res], writes=[self.r[i]])
        return self.t[i], self.r[i]


def tile_pair(wa, wb):
    K = wa.shape[0] // 128
    M = wa.shape[1] // 128
    a = wa.reshape(K, 128, M, 128).transpose(2, 1, 0, 3)
    b = wb.reshape(K, 128, M, 128).transpose(2, 1, 0, 3)
    return np.ascontiguousarray(np.stack([a, b], axis=2).reshape(M, 128, 2 * K * 128))


def tile_single(w):
    K = w.shape[0] // 128
    M = w.shape[1] // 128
    return np.ascontiguousarray(w.reshape(K, 128, M, 128).transpose(2, 1, 0, 3).reshape(M, 128, K * 128))


def tile_rows(w, per=2):
    K = w.shape[0] // 128
    N = w.shape[1]
    return np.ascontiguousarray(w.reshape(K // per, per, 128, N).transpose(0, 2, 1, 3).reshape(K // per, 128, per * N))


class Ctx:
    pass


class WGroup:
    def __init__(self, nc, name, nslabs):
        self.name = name
        self.nslabs = nslabs
        self.rows = nslabs * 128 // NCORES
        self.win = nc.dram_tensor("w_" + name, [self.rows, SLOT], F32, kind="ExternalInput").ap()
        self.cin = nc.dram_tensor("ci_" + name, [self.rows, SLOT], BF16)
        self.cout = nc.dram_tensor("co_" + name, [nslabs * 128, SLOT], BF16)
        self.cin_r = Res("cin_" + name)
        self.cout_r = Res("cout_" + name)

    def slab(self, i):
        return self.cout.ap()[i * 128:(i + 1) * 128, :]


def build(NT, layers, dbg=False):
    nc = bass.Bass("TRN2", target_bir_lowering=False)
    S = Sched(nc)
    NTL = NT // T
    C = Ctx()
    C.nc, C.S, C.NT, C.NTL = nc, S, NT, NTL

    xT = nc.dram_tensor("xT", [D, NT], F32, kind="ExternalInput").ap()
    outT = nc.dram_tensor("outT", [D, NT], F32, kind="ExternalOutput").ap()
    ngT = nc.dram_tensor("ngT", [128, 12 * KC], F32, kind="ExternalInput").ap()
    NSLAB = {"ffn": 33, "conv": 16, "sg": 12, "mla": 8}
    groups = [WGroup(nc, f"g{i}", NSLAB[l[0]]) for i, l in enumerate(layers)]
    kinds = [l[0] for l in layers]
    cmaskT = nc.dram_tensor("cmask", [128, 2], F32, kind="ExternalInput").ap()
    if "conv" in kinds:
        ckT = nc.dram_tensor("ckT", [128, 2 * 24], F32, kind="ExternalInput").ap()
        zc_in = nc.dram_tensor("zc_in", [128, 16], F32)
        zc_out = nc.dram_tensor("zc_out", [256, 16], F32)
    if "mla" in kinds:
        SQ = 2 * NT
        posT = nc.dram_tensor("pos32", [32, SQ], I32, kind="ExternalInput").ap()
        ropec = nc.dram_tensor("ropec", [32, 2], F32, kind="ExternalInput").ap()
        mlag = nc.dram_tensor("mlag", [128, 8], F32, kind="ExternalInput").ap()
        mlac = nc.dram_tensor("mlac", [128, 256 + 64], F32, kind="ExternalInput").ap()
        mlahw = nc.dram_tensor("mlahw", [128, 3072], F32, kind="ExternalInput").ap()
        cs_tab = nc.dram_tensor("cs_tab", [64, SQ], F32)
        LC = min(NT, 1024)
        NLC = NT // LC
        lat_in = [nc.dram_tensor(f"lat_in{j}", [416, LC], BF16) for j in range(NLC)]
        lat_out = [nc.dram_tensor(f"lat_out{j}", [832, LC], BF16) for j in range(NLC)]
        o_in = [nc.dram_tensor(f"o_in{h}", [64, SQ], BF16) for h in range(8)]
        o_out = [nc.dram_tensor(f"o_out{h}", [128, SQ], BF16) for h in range(8)]
    if "sg" in kinds:
        sgvnT = nc.dram_tensor("sgvn", [128, D], F32, kind="ExternalInput").ap()
        wsT = nc.dram_tensor("wsT", [128, D], F32, kind="ExternalInput").ap()
        mask01 = nc.dram_tensor("mask01", [128, 128], F32, kind="ExternalInput").ap()
        sgbT = nc.dram_tensor("sgb", [1, D], F32, kind="ExternalInput").ap()

    base0 = nc.sbuf_base
    off = [(base0 + 63) // 64 * 64]

    def sb(name, shape, dt):
        n = 1
        for d_ in shape[1:]:
            n *= d_
        nb = n * (4 if dt in (F32, I32) else 2)
        t_ = nc.alloc_sbuf_tensor_at(name, shape, dt, offset=off[0])
        off[0] += (nb + 63) // 64 * 64
        return t_
    xres = sb("xres", [128, KC, NT], F32)
    xr = [[Res(f"x{t}_{c}") for c in range(KC)] for t in range(NTL)]
    ng = sb("ng", [128, 12 * KC], F32)
    ng_r = Res("ng")
    ones_bf = sb("ones_bf", [128, 128], BF16)
    ones_r = Res("ones")
    cmask = sb("cmask_sb", [128, 2], F32)
    cmask_r = Res("cmask")
    if "conv" in kinds:
        ck = sb("ck", [128, 48], F32)
        ck_r = Res("ck")
    if "mla" in kinds:
        mg = sb("mg", [128, 8], F32)
        mc16 = sb("mc16", [128, 256], BF16)
        sel65 = sb("sel65", [128, 64], F32)
        rpc = sb("rpc", [32, 2], F32)
        mconst_r = Res("mconst")
    xn_off = off[0]
    xn = [sb(f"xn{i}", [128, KC, T], BF16) for i in range(2)]
    xn_r = [Res(f"xn{i}") for i in range(2)]
    sq = [sb(f"sq{i}", [128, T], BF16) for i in range(2)]
    sq_r = [Res(f"sq{i}") for i in range(2)]
    rs = sb("rs", [128, T], F32)
    rs_r = Res("rs")
    sg_ss = sb("sg_ss", [128, 8], F32)
    ss_r = Res("ss")
    ring = Ring(S, nc, 5, sb)
    st32 = [sb(f"st32_{i}", [128, 512], F32) for i in range(2)]
    st16 = [sb(f"st16_{i}", [128, 512], BF16) for i in range(2)]
    st32_r = [Res(f"st32_{i}") for i in range(2)]
    st16_r = [Res(f"st16_{i}") for i in range(2)]
    arena0 = off[0]
    ARENA = 29696
    off[0] += ARENA
    assert off[0] <= nc.sbuf_top, (off[0], nc.sbuf_top)

    def arena_alloc():
        o = [arena0]

        def f(name, shape, dt):
            n = 1
            for d_ in shape[1:]:
                n *= d_
            nb = n * (4 if dt in (F32, I32) else 2)
            t_ = nc.alloc_sbuf_tensor_at(name, shape, dt, offset=o[0])
            o[0] += (nb + 63) // 64 * 64
            assert o[0] <= arena0 + ARENA, (name, o[0] - arena0)
            return t_
        return f
    ps = [nc.alloc_psum_tensor(f"ps{i}", [128, T], F32) for i in range(8)]
    ps_r = [Res(f"ps{i}") for i in range(8)]
    C.xres, C.xr, C.ng, C.ng_r, C.ones_bf, C.ones_r = xres, xr, ng, ng_r, ones_bf, ones_r
    C.xn, C.xn_r, C.sq, C.sq_r, C.rs, C.rs_r = xn, xn_r, sq, sq_r, rs, rs_r
    C.ring, C.ps, C.ps_r = ring, ps, ps_r

    S.memset(ones_bf[:], 1.0, [ones_r])
    S.dma(ng[:], ngT, writes=[ng_r])
    S.dma(cmask[:], cmaskT, writes=[cmask_r])
    if "conv" in kinds:
        S.dma(ck[:], ckT, writes=[ck_r])
    xTv = xT.rearrange("(c p) n -> p c n", p=128)
    for t in range(NTL):
        S.dma(xres[:, :, t * T:(t + 1) * T], xTv[:, :, t * T:(t + 1) * T], writes=xr[t])

    stc = [0]

    def prefetch_tasks(g):
        tasks = []
        for r0 in range(0, g.rows, 128):
            nr = min(128, g.rows - r0)
            for c0 in range(0, SLOT, 512):
                def task(r0=r0, nr=nr, c0=c0):
                    i = stc[0] % 2
                    stc[0] += 1
                    S.dma(st32[i][0:nr, :], g.win[r0:r0 + nr, c0:c0 + 512], writes=[st32_r[i]])
                    S.copy(st16[i][0:nr, :], st32[i][0:nr, :], [st32_r[i]], [st16_r[i]], eng="pool")
                    S.dma(g.cin.ap()[r0:r0 + nr, c0:c0 + 512], st16[i][0:nr, :], reads=[st16_r[i]], writes=[g.cin_r])
                tasks.append(task)

        def coll():
            S.collective(lambda e: e.collective_compute(
                "AllGather", ALU.bypass, replica_groups=[list(range(NCORES))],
                ins=[g.cin.ap().opt()], outs=[g.cout.ap().opt()]), reads=[g.cin_r], writes=[g.cout_r])
        tasks.append(coll)
        return tasks
    def ring_load_n(src, src_r, _unused):
        return ring.load(src, src_r)
    bg = []

    def tick(n=1):
        for _ in range(n):
            if bg:
                bg.pop(0)()

    def drain_bg():
        while bg:
            bg.pop(0)()

    def norm_a(t, nbank):
        for c in range(KC):
            S.act(sq[c % 2][:], xres[:, c, t * T:(t + 1) * T], AF.Square, [xr[t][c]], [sq_r[c % 2]])
            S.mm(ps[nbank][:], ones_bf[:], sq[c % 2][:], c == 0, c == KC - 1, [sq_r[c % 2], ones_r], [ps_r[nbank]])

    def norm_b(t, nbank, gi, dst, dst_r):
        S.act(rs[:], ps[nbank][:], AF.Ln, [ps_r[nbank]], [rs_r], scale=1.0 / D, bias=EPS)
        S.act(rs[:], rs[:], AF.Exp, [rs_r], [rs_r], scale=-0.5)
        for c in range(KC):
            S.stt(dst[:, c, :], xres[:, c, t * T:(t + 1) * T], ng[:, gi * KC + c:gi * KC + c + 1], rs[:],
                  ALU.mult, ALU.mult, [xr[t][c], ng_r, rs_r], [dst_r])

    def ffn(g, gi, coef):
        al = arena_alloc()
        hbuf = al("hbuf", [128, FC, T], BF16)
        h_r = [Res(f"h{m}") for m in range(FC)]
        sg = [al(f"sg{i}", [128, T], F32) for i in range(2)]
        sg_r = [Res(f"sg{i}") for i in range(2)]
        norm_a(0, 7)
        norm_b(0, 7, gi, xn[0], xn_r[0])
        for t in range(NTL):
            cur = xn[t % 2]
            cur_r = xn_r[t % 2]
            for m in range(FC):
                w, w_r = ring.load(g.slab(m), g.cout_r)
                gb, ub = (2 * m) % 4, (2 * m + 1) % 4
                for half, bank in ((0, gb), (1, ub)):
                    for k in range(KC):
                        o = (half * KC + k) * 128
                        S.mm(ps[bank][:], w[:, o:o + 128], cur[:, k, :], k == 0, k == KC - 1, [w_r, cur_r], [ps_r[bank]])
                S.act(sg[m % 2][:], ps[gb][:], AF.Silu, [ps_r[gb]], [sg_r[m % 2]])
                S.tt(hbuf[:, m, :], sg[m % 2][:], ps[ub][:], ALU.mult, [sg_r[m % 2], ps_r[ub]], [h_r[m]])
                tick()
                if t + 1 < NTL:
                    if m == 4:
                        norm_a(t + 1, 7)
                    if m == 8:
                        norm_b(t + 1, 7, gi, xn[(t + 1) % 2], xn_r[(t + 1) % 2])
            corder = [4, 5, 6, 7, 0, 1, 2, 3]
            for kk in range(FC // 2):
                w, w_r = ring.load(g.slab(FC + kk), g.cout_r)
                for i in range(2):
                    k = 2 * kk + i
                    for c in corder:
                        o = i * D + c * 128
                        S.mm(ps[c][:], w[:, o:o + 128], hbuf[:, k, :], k == 0, k == FC - 1, [w_r, h_r[k]], [ps_r[c]])
                tick()
            for c in corder:
                xs = xres[:, c, t * T:(t + 1) * T]
                S.stt(xs, ps[c][:], float(coef), xs, ALU.mult, ALU.add, [ps_r[c], xr[t][c]], [xr[t][c]])


    def out_proj(g, slab0, y, y_r, t, coef):
        corder = [4, 5, 6, 7, 0, 1, 2, 3]
        for kk in range(KC // 2):
            w, w_r = ring.load(g.slab(slab0 + kk), g.cout_r)
            for i in range(2):
                k = 2 * kk + i
                for c in corder:
                    o = i * D + c * 128
                    S.mm(ps[c][:], w[:, o:o + 128], y[:, k, :], k == 0, k == KC - 1, [w_r] + y_r, [ps_r[c]])
            tick()
        for c in corder:
            xs = xres[:, c, t * T:(t + 1) * T]
            S.stt(xs, ps[c][:], float(coef), xs, ALU.mult, ALU.add, [ps_r[c], xr[t][c]], [xr[t][c]])

    def conv(g, gi, ci):
        al = arena_alloc()
        zbuf = al("zbuf", [128, KC, T + 2], F32)
        z_r = [Res(f"z{c}") for c in range(KC)]
        ybuf = al("ybuf", [128, KC, T], BF16)
        y_r = [Res(f"y{c}") for c in range(KC)]
        cg = al("cg", [128, T], F32)
        cg_r = Res("cg")
        acc = al("acc", [128, T], F32)
        acc_r = Res("acc")
        xm = al("xm", [128, KC, 2], BF16)
        xm_r = Res("xm")
        small = al("small", [128, 64], F32)
        small_r = Res("small")
        kb = ci * 24
        for c in range(KC):
            S.act(sq[0][:, 2 * c:2 * c + 2], xres[:, c, NT - 2:NT], AF.Square, [xr[NTL - 1][c]], [sq_r[0]])
        for c in range(KC):
            S.mm(ps[7][:, 0:2], ones_bf[:], sq[0][:, 2 * c:2 * c + 2], c == 0, c == KC - 1, [sq_r[0], ones_r], [ps_r[7]])
        S.act(small[:, 0:2], ps[7][:, 0:2], AF.Ln, [ps_r[7]], [small_r], scale=1.0 / D, bias=EPS)
        S.act(small[:, 0:2], small[:, 0:2], AF.Exp, [small_r], [small_r], scale=-0.5)
        for c in range(KC):
            S.stt(xm[:, c, :], xres[:, c, NT - 2:NT], ng[:, gi * KC + c:gi * KC + c + 1], small[:, 0:2],
                  ALU.mult, ALU.mult, [xr[NTL - 1][c], ng_r, small_r], [xm_r])
        for c in range(KC):
            w, w_r = ring.load(g.slab(c), g.cout_r)
            for half, bank in ((0, 5), (1, 6)):
                for k in range(KC):
                    o = (half * KC + k) * 128
                    S.mm(ps[bank][:, 0:2], w[:, o:o + 128], xm[:, k, :], k == 0, k == KC - 1, [w_r, xm_r], [ps_r[bank]])
            S.act(cg[:, 0:2], ps[5][:, 0:2], AF.Copy, [ps_r[5]], [cg_r])
            S.tt(small[:, 16 + 2 * c:18 + 2 * c], cg[:, 0:2], ps[6][:, 0:2], ALU.mult, [cg_r, ps_r[6]], [small_r])
        zin_r, zout_r = Res("zin"), Res("zout")
        S.dma(zc_in.ap(), small[:, 16:32], reads=[small_r], writes=[zin_r])
        S.collective(lambda e: e.collective_compute(
            "AllGather", ALU.bypass, replica_groups=[[0, 1], [2, 3], [4, 5], [6, 7]],
            ins=[zc_in.ap().opt()], outs=[zc_out.ap().opt()]), reads=[zin_r], writes=[zout_r])
        S.dma(small[:, 32:48], zc_out.ap()[0:128, :], reads=[zout_r], writes=[small_r])
        S.ts(zbuf[:, :, 0:2], small[:, 32:48].rearrange("p (c j) -> p c j", j=2), cmask[:, 1:2], None, ALU.mult, ALU.bypass,
             [small_r, cmask_r], z_r)
        norm_a(0, 7)
        norm_b(0, 7, gi, xn[0], xn_r[0])
        for t in range(NTL):
            cur, cur_r = xn[t % 2], xn_r[t % 2]
            if t > 0:
                S.copy(zbuf[:, :, 0:2], zbuf[:, :, T:T + 2], z_r, z_r)
            for c in range(KC):
                w, w_r = ring.load(g.slab(c), g.cout_r)
                for half, bank in ((0, 0), (1, 1)):
                    for k in range(KC):
                        o = (half * KC + k) * 128
                        S.mm(ps[bank][:], w[:, o:o + 128], cur[:, k, :], k == 0, k == KC - 1, [w_r, cur_r], [ps_r[bank]])
                if c % 2 == 0:
                    w2, w2_r = ring.load(g.slab(8 + c // 2), g.cout_r)
                for k in range(KC):
                    o = ((c % 2) * KC + k) * 128
                    S.mm(ps[2][:], w2[:, o:o + 128], cur[:, k, :], k == 0, k == KC - 1, [w2_r, cur_r], [ps_r[2]])
                S.act(cg[:], ps[0][:], AF.Copy, [ps_r[0]], [cg_r])
                S.tt(zbuf[:, c, 2:T + 2], cg[:], ps[1][:], ALU.mult, [cg_r, ps_r[1]], [z_r[c]])
                S.ts(acc[:], zbuf[:, c, 2:T + 2], ck[:, kb + 16 + c:kb + 17 + c], None, ALU.mult, ALU.bypass, [z_r[c], ck_r], [acc_r])
                S.stt(acc[:], zbuf[:, c, 1:T + 1], ck[:, kb + 8 + c:kb + 9 + c], acc[:], ALU.mult, ALU.add, [z_r[c], ck_r, acc_r], [acc_r])
                S.stt(acc[:], zbuf[:, c, 0:T], ck[:, kb + c:kb + 1 + c], acc[:], ALU.mult, ALU.add, [z_r[c], ck_r, acc_r], [acc_r])
                S.tt(ybuf[:, c, :], acc[:], ps[2][:], ALU.mult, [acc_r, ps_r[2]], [y_r[c]])
                tick()
                if t + 1 < NTL:
                    if c == 2:
                        norm_a(t + 1, 7)
                    if c == 4:
                        norm_b(t + 1, 7, gi, xn[(t + 1) % 2], xn_r[(t + 1) % 2])
            out_proj(g, 12, ybuf, y_r, t, 1.0)


    def mla(g, gi):
        SQ = 2 * NT
        NS = SQ // T
        scale = 96.0 ** -0.5
        barrier = S.barrier
        al = arena_alloc()
        stg = al("stg", [128, 3072], F32)
        stg_r = Res("stg")
        posi = al("posi", [32, T], I32)
        posf = al("posf", [32, T], F32)
        kf = al("kf", [32, T], F32)
        ki = al("ki", [32, T], I32)
        rr = al("rr", [32, T], F32)
        cst = al("cst", [32, 2, T], F32)
        tr = Res("ropetmp")
        S.dma(mg[:], mlag, writes=[mconst_r])
        S.dma(rpc[:], ropec, writes=[mconst_r])
        S.dma(stg[:, 0:320], mlac, writes=[stg_r])
        S.copy(mc16[:], stg[:, 0:256], [stg_r], [mconst_r])
        S.copy(sel65[:], stg[:, 256:320], [stg_r], [mconst_r])
        barrier()
        cs_r = Res("cs_tab")
        TWO_PI = 2.0 * np.pi
        c1 = float(np.float32(6.28125))
        c2 = float(np.float32(TWO_PI - 6.28125))
        c3 = float(TWO_PI - 6.28125 - np.float64(np.float32(TWO_PI - 6.28125)))
        for st in range(NS):
            S.dma(posi[:], posT[:, st * T:(st + 1) * T], writes=[tr])
            S.copy(posf[:], posi[:], [tr], [tr])
            S.ts(posf[:], posf[:], rpc[:, 0:1], None, ALU.mult, ALU.bypass, [tr, mconst_r], [tr])
            S.ts(kf[:], posf[:], float(1.0 / TWO_PI), None, ALU.mult, ALU.bypass, [tr], [tr])
            S.copy(ki[:], kf[:], [tr], [tr])
            S.copy(kf[:], ki[:], [tr], [tr])
            S.stt(rr[:], kf[:], -c1, posf[:], ALU.mult, ALU.add, [tr], [tr])
            S.stt(rr[:], kf[:], -c2, rr[:], ALU.mult, ALU.add, [tr], [tr])
            S.stt(rr[:], kf[:], -c3, rr[:], ALU.mult, ALU.add, [tr], [tr])
            for ti, shift in ((0, float(np.pi / 2)), (1, 0.0)):
                y = cst[:, ti, :]
                S.ts(y, rr[:], shift, None, ALU.add, ALU.bypass, [tr], [tr])
                S.ts(kf[:], y, float(np.pi), None, ALU.is_gt, ALU.bypass, [tr], [tr])
                S.stt(y, kf[:], -TWO_PI, y, ALU.mult, ALU.add, [tr], [tr])
                S.ts(kf[:], y, -float(np.pi), None, ALU.is_lt, ALU.bypass, [tr], [tr])
                S.stt(y, kf[:], TWO_PI, y, ALU.mult, ALU.add, [tr], [tr])
            S.act(cst[:, 0, :], cst[:, 0, :], AF.Sin, [tr], [tr])
            S.act(cst[:, 1, :], cst[:, 1, :], AF.Sin, [tr], [tr])
            S.ts(cst[:, 1, :], cst[:, 1, :], rpc[:, 1:2], None, ALU.mult, ALU.bypass, [tr, mconst_r], [tr])
            S.dma(cs_tab.ap()[0:32, st * T:(st + 1) * T], cst[:, 0, :], reads=[tr], writes=[cs_r])
            S.dma(cs_tab.ap()[32:64, st * T:(st + 1) * T], cst[:, 1, :], reads=[tr], writes=[cs_r])
        barrier()

        al = arena_alloc()
        lq = al("lq", [128, 2, T], BF16)
        lkv = al("lkv", [128, T], BF16)
        lpe = al("lpe", [32, T], BF16)
        sqa = al("sqa", [128, T], BF16)
        sqb_ = al("sqb", [128, T], BF16)
        r1 = al("r1", [128, T], F32)
        m1r = Res("m1")
        lat_in_r = [Res(f"lat_in{j}") for j in range(NLC)]
        lat_out_r = [Res(f"lat_out{j}") for j in range(NLC)]
        TPC = LC // T
        norm_a(0, 7)
        norm_b(0, 7, gi, xn[0], xn_r[0])
        for t in range(NTL):
            cur, cur_r = xn[t % 2], xn_r[t % 2]
            ws = [ring_load_n(g.slab(i), g.cout_r, i + 2) for i in range(2)]
            for (c0, m, bank) in ((0, 128, 0), (128, 128, 1), (256, 128, 2), (384, 32, 3)):
                for k in range(KC):
                    w, w_r = ws[k // 4]
                    o = (k % 4) * 416 + c0
                    S.mm(ps[bank][0:m, :], w[:, o:o + m], cur[:, k, :], k == 0, k == KC - 1, [w_r, cur_r], [ps_r[bank]])
            S.act(sqa[:], ps[0][:], AF.Square, [ps_r[0]], [m1r])
            S.act(sqb_[:], ps[1][:], AF.Square, [ps_r[1]], [m1r])
            S.mm(ps[4][:], ones_bf[:], sqa[:], True, False, [m1r, ones_r], [ps_r[4]])
            S.mm(ps[4][:], ones_bf[:], sqb_[:], False, True, [m1r, ones_r], [ps_r[4]])
            S.act(r1[:], ps[4][:], AF.Ln, [ps_r[4]], [m1r], scale=1.0 / 256, bias=EPS)
            S.act(r1[:], r1[:], AF.Exp, [m1r], [m1r], scale=-0.5)
            for j in range(2):
                S.stt(lq[:, j, :], ps[j][:], mg[:, 4 + j:5 + j], r1[:], ALU.mult, ALU.mult, [ps_r[j], m1r, mconst_r], [m1r])
            lj, lc0 = t // TPC, (t % TPC) * T
            S.dma(lat_in[lj].ap()[0:256, lc0:lc0 + T].rearrange("(j p) n -> p j n", p=128), lq[:], reads=[m1r], writes=[lat_in_r[lj]])
            S.act(sqa[:], ps[2][:], AF.Square, [ps_r[2], m1r], [m1r])
            S.mm(ps[5][:], ones_bf[:], sqa[:], True, True, [m1r, ones_r], [ps_r[5]])
            S.act(r1[:], ps[5][:], AF.Ln, [ps_r[5], m1r], [m1r], scale=1.0 / 128, bias=EPS)
            S.act(r1[:], r1[:], AF.Exp, [m1r], [m1r], scale=-0.5)
            S.stt(lkv[:], ps[2][:], mg[:, 6:7], r1[:], ALU.mult, ALU.mult, [ps_r[2], m1r, mconst_r], [m1r])
            S.dma(lat_in[lj].ap()[256:384, lc0:lc0 + T], lkv[:], reads=[m1r], writes=[lat_in_r[lj]])
            S.copy(lpe[:], ps[3][0:32, :], [ps_r[3], m1r], [m1r])
            S.dma(lat_in[lj].ap()[384:416, lc0:lc0 + T], lpe[:], reads=[m1r], writes=[lat_in_r[lj]])
            if (t + 1) % TPC == 0:
                S.collective(lambda e, lj=lj: e.collective_compute(
                    "AllGather", ALU.bypass, replica_groups=[[0, 1], [2, 3], [4, 5], [6, 7]],
                    ins=[lat_in[lj].ap().opt()], outs=[lat_out[lj].ap().opt()]), reads=[lat_in_r[lj]], writes=[lat_out_r[lj]])
            tick(4)
            if t + 1 < NTL:
                norm_a(t + 1, 7)
                norm_b(t + 1, 7, gi, xn[(t + 1) % 2], xn_r[(t + 1) % 2])
        barrier()

        al = arena_alloc()
        stg = al("stg2", [128, 3072], F32)
        S.dma(stg[:], mlahw, writes=[stg_r])
        hq, hq_r = ring.t[0], ring.r[0]
        hkv, hkv_r = ring.t[1], ring.r[1]
        S.copy(hq[:], stg[:, 0:2048], [stg_r], [hq_r])
        S.copy(hkv[:, 0:1024], stg[:, 2048:3072], [stg_r], [hkv_r])
        barrier()
        al = arena_alloc()
        KT = al("KT", [128, SQ], BF16)
        Vh = al("Vh", [128, SQ // 128, 65], BF16)
        Oa = al("Oa", [128, T], F32)
        rl = al("rl", [128, T], F32)
        xo = [xn_off]

        def xa(name, shape, dt):
            n = 1
            for d_ in shape[1:]:
                n *= d_
            nb = n * (4 if dt in (F32, I32) else 2)
            t_ = nc.alloc_sbuf_tensor_at(name, shape, dt, offset=xo[0])
            xo[0] += nb
            assert xo[0] <= xn_off + 16384
            return t_
        pt = [xa(f"pt{i}", [128, T], BF16) for i in range(2)]
        qts = xa("qts", [128, T], BF16)
        sqh = xa("sqh", [128, T], BF16)
        cqt = xa("cqt", [128, 2, T], BF16)
        ckvt = xa("ckvt", [128, T], BF16)
        kpet = xa("kpet", [128, T], BF16)
        Ct = xa("Ct", [128, T], F32)
        St = xa("St", [128, T], F32)
        tA = xa("tA", [128, T], F32)
        tB = xa("tB", [128, T], F32)
        on = sq[0]
        R = rs
        kt_r, vh_r, pt_r = Res("KT"), Res("Vh"), [Res("pt0"), Res("pt1")]
        qts_r, w_r_ = Res("qts"), Res("work")
        in_r = Res("mla_in")
        cs_in_r = Res("cs_in")
        oin_r = [Res(f"o_in{h}") for h in range(8)]
        oout_r = [Res(f"o_out{h}") for h in range(8)]
        onr = Res("on")
        S.memset(Vh[:, :, 64:65], 1.0, [vh_r])
        selI = mc16[0:32, 0:96]
        selSw = mc16[0:32, 96:128]
        m01 = mc16[:, 128:256]

        def lat_cols(st):
            r_, lt = st // NTL, st % NTL
            return r_ * 416, (lt % TPC) * T, lt // TPC

        def norm_rope(A, Bk, gcol, dst, dst_r, extra_scale):
            S.act(sqh[0:96, :], ps[A][0:96, :], AF.Square, [ps_r[A]], [w_r_])
            S.mm(ps[6][0:96, :], ones_bf[0:96, 0:96], sqh[0:96, :], True, True, [w_r_, ones_r], [ps_r[6]])
            S.act(R[0:96, :], ps[6][0:96, :], AF.Ln, [ps_r[6]], [w_r_], scale=1.0 / 96, bias=EPS)
            S.act(R[0:96, :], R[0:96, :], AF.Exp, [w_r_], [w_r_], scale=-0.5)
            S.stt(dst[0:64, :], ps[A][0:64, :], mg[0:64, gcol:gcol + 1], R[0:64, :], ALU.mult, ALU.mult,
                  [ps_r[A], w_r_, mconst_r], [dst_r])
            S.stt(tA[64:96, :], ps[A][64:96, :], mg[64:96, gcol:gcol + 1], Ct[64:96, :], ALU.mult, ALU.mult,
                  [ps_r[A], cs_in_r, mconst_r], [w_r_])
            S.stt(tB[64:96, :], ps[Bk][64:96, :], mg[64:96, gcol + 1:gcol + 2], St[64:96, :], ALU.mult, ALU.mult,
                  [ps_r[Bk], cs_in_r, mconst_r], [w_r_])
            S.tt(tA[64:96, :], tA[64:96, :], tB[64:96, :], ALU.add, [w_r_], [w_r_])
            S.tt(dst[64:96, :], tA[64:96, :], R[64:96, :], ALU.mult, [w_r_], [dst_r])

        def load_cs(st):
            S.dma(Ct[64:96, :], cs_tab.ap()[0:32, st * T:(st + 1) * T], reads=[cs_r], writes=[cs_in_r])
            S.dma(St[64:96, :], cs_tab.ap()[32:64, st * T:(st + 1) * T], reads=[cs_r], writes=[cs_in_r])

        for hl in range(8):
            wq = hq[:, hl * 256:(hl + 1) * 256]
            wk = hkv[:, hl * 128:hl * 128 + 64]
            wv = hkv[:, hl * 128 + 64:hl * 128 + 128]
            for st in range(NS):
                r0, c0, lj = lat_cols(st)
                S.dma(ckvt[:], lat_out[lj].ap()[r0 + 256:r0 + 384, c0:c0 + T], reads=[lat_out_r[lj]], writes=[in_r])
                S.dma(kpet[0:32, :], lat_out[lj].ap()[r0 + 384:r0 + 416, c0:c0 + T], reads=[lat_out_r[lj]], writes=[in_r])
                load_cs(st)
                S.mm(ps[0][0:64, :], wk, ckvt[:], True, True, [hkv_r, in_r], [ps_r[0]])
                S.mm(ps[0][64:96, :], selI[:, 64:96], kpet[0:32, :], True, True, [mconst_r, in_r], [ps_r[0]])
                S.mm(ps[1][64:96, :], selSw, kpet[0:32, :], True, True, [mconst_r, in_r], [ps_r[1]])
                norm_rope(0, 1, 2, KT[:, st * T:(st + 1) * T], kt_r, 1.0)
                for jb in range(4):
                    S.mm(ps[2][:, jb * 64:(jb + 1) * 64], ckvt[:, jb * 128:(jb + 1) * 128], wv, True, True, [in_r, hkv_r], [ps_r[2]])
                S.act(Vh[:, st * 4:(st + 1) * 4, 0:64], ps[2][:, 0:256].rearrange("p (j d) -> p j d", d=64), AF.Copy, [ps_r[2]], [vh_r])
                tick()
            for qt in range(NS):
                r0, c0, lj = lat_cols(qt)
                S.dma(cqt[:], lat_out[lj].ap()[r0:r0 + 256, c0:c0 + T].rearrange("(j p) n -> p j n", p=128), reads=[lat_out_r[lj]], writes=[in_r])
                load_cs(qt)
                for kc in range(2):
                    S.mm(ps[0][0:96, :], wq[:, kc * 128:kc * 128 + 96], cqt[:, kc, :], kc == 0, kc == 1, [hq_r, in_r], [ps_r[0]])
                for kc in range(2):
                    S.mm(ps[1][64:96, :], wq[:, kc * 128 + 96:kc * 128 + 128], cqt[:, kc, :], kc == 0, kc == 1, [hq_r, in_r], [ps_r[1]])
                norm_rope(0, 1, 0, qts, qts_r, 1.0)
                nkb = qt * 4 + 4
                for kb in range(nkb):
                    j = kb - qt * 4
                    cs0 = 0 if j <= 0 else j * 128
                    sb_ = 2 + kb % 2
                    S.mm(ps[sb_][:, cs0:T], KT[0:96, kb * 128:(kb + 1) * 128], qts[0:96, cs0:T], True, True, [kt_r, qts_r], [ps_r[sb_]])
                    p_, p_r = pt[kb % 2], pt_r[kb % 2]
                    S.act(p_[:, cs0:T], ps[sb_][:, cs0:T], AF.Exp, [ps_r[sb_]], [p_r], scale=scale)
                    if j >= 0:
                        S.tt(p_[:, cs0:cs0 + 128], p_[:, cs0:cs0 + 128], m01, ALU.mult, [p_r, mconst_r], [p_r], eng="pool")
                    S.mm(ps[4][0:65, cs0:T], Vh[:, kb, :], p_[:, cs0:T], kb == 0, kb == nkb - 1, [vh_r, p_r], [ps_r[4]])
                S.copy(Oa[0:65, :], ps[4][0:65, :], [ps_r[4]], [w_r_])
                S.mm(ps[5][0:64, :], sel65[0:65, :], Oa[0:65, :], True, True, [w_r_, mconst_r], [ps_r[5]])
                S.op("dve", lambda e: e.reciprocal(rl[0:64, :], ps[5][0:64, :]), [ps_r[5]], [w_r_])
                S.tt(on[0:64, :], Oa[0:64, :], rl[0:64, :], ALU.mult, [w_r_, onr], [onr])
                S.dma(o_in[hl].ap()[:, qt * T:(qt + 1) * T], on[0:64, :], reads=[onr], writes=[oin_r[hl]])
                tick()
            S.collective(lambda e, hl=hl: e.collective_compute(
                "AllGather", ALU.bypass, replica_groups=[[0, 1], [2, 3], [4, 5], [6, 7]],
                ins=[o_in[hl].ap().opt()], outs=[o_out[hl].ap().opt()]), reads=[oin_r[hl]], writes=[oout_r[hl]])
        barrier()

        al = arena_alloc()
        cand = [al(f"cand{i}", [128, KC, T], BF16) for i in range(2)]
        yb = al("yb", [128, KC, T], BF16)
        tmp = al("tmp3", [128, T], F32)
        cand_r, y_r3, tmp_r = Res("cand"), [Res(f"y3_{c}") for c in range(KC)], Res("tmp3")
        for t in range(NTL):
            for i in range(2):
                for gh in range(16):
                    r_, hl_ = gh // 8, gh % 8
                    S.dma(cand[i][(gh % 2) * 64:(gh % 2) * 64 + 64, gh // 2, :],
                          o_out[hl_].ap()[r_ * 64:(r_ + 1) * 64, i * NT + t * T:i * NT + (t + 1) * T],
                          reads=[oout_r[hl_]], writes=[cand_r])
            for c in range(KC):
                S.ts(tmp[:], cand[0][:, c, :], cmask[:, 0:1], None, ALU.mult, ALU.bypass, [cand_r, cmask_r], [tmp_r])
                S.stt(yb[:, c, :], cand[1][:, c, :], cmask[:, 1:2], tmp[:], ALU.mult, ALU.add, [cand_r, cmask_r, tmp_r], [y_r3[c]])
            out_proj(g, 2, yb, y_r3, t, 1.0)
        barrier()

    def sgm(g, gi):
        al = arena_alloc()
        ub = al("ub", [128, KC, T], BF16)
        u_r = [Res(f"u{c}") for c in range(KC)]
        vn = al("vn", [128, 4, D], BF16)
        vn_r = [Res(f"vn{j}") for j in range(4)]
        vg = al("vg", [128, D], F32)
        vg_r = Res("vg")
        wm = al("wm", [128, D], BF16)
        wm_r = Res("wm")
        gvn = al("gvn", [128, D], F32)
        gvn_r = Res("gvn")
        bsb = al("bsb", [128, D], BF16)
        bsb_r = Res("bsb")
        S.dma(gvn[:], sgvnT, writes=[gvn_r])
        S.dma(vg[:], wsT, writes=[vg_r])
        S.dma(rs[:, 0:128], mask01, writes=[rs_r])
        for gg in range(8):
            S.tt(wm[:, gg * 128:(gg + 1) * 128], vg[:, gg * 128:(gg + 1) * 128], rs[:, 0:128], ALU.mult, [vg_r, rs_r], [wm_r])
        S.barrier()
        S.dma(vg[0:1, :], sgbT, writes=[vg_r])
        S.memset(bsb[:], 0.0, [bsb_r])
        S.copy(bsb[0:1, :], vg[0:1, :], [vg_r, bsb_r], [bsb_r])
        S.barrier()
        norm_a(0, 7)
        norm_b(0, 7, gi, xn[0], xn_r[0])
        for t in range(NTL):
            cur, cur_r = xn[t % 2], xn_r[t % 2]
            for cc in range(4):
                w, w_r = ring.load(g.slab(cc), g.cout_r)
                for j in range(2):
                    c = 2 * cc + j
                    for k in range(KC):
                        o = (j * KC + k) * 128
                        S.mm(ps[c % 4][:], w[:, o:o + 128], cur[:, k, :], k == 0, k == KC - 1, [w_r, cur_r], [ps_r[c % 4]])
                    S.act(ub[:, c, :], ps[c % 4][:], AF.Gelu, [ps_r[c % 4]], [u_r[c]])
                tick()
            for kk in range(4):
                w, w_r = ring.load(g.slab(4 + kk), g.cout_r)
                for i in range(2):
                    k = 2 * kk + i
                    for j in range(4):
                        for hf in range(2):
                            S.mm(ps[j * 2 + hf][:], cur[:, k, j * 128:(j + 1) * 128], w[:, i * D + hf * 512:i * D + hf * 512 + 512],
                                 k == 0, k == KC - 1, [w_r, cur_r], [ps_r[j * 2 + hf]])
                tick()
            for j in range(4):
                for hf in range(2):
                    S.act(vg[:, hf * 512:(hf + 1) * 512], ps[j * 2 + hf][:], AF.Gelu, [ps_r[j * 2 + hf]], [vg_r])
                S.op("act", lambda e, j=j: e.activation(out=vn[:, j, :], in_=vg[:], func=AF.Square, accum_out=sg_ss[:, j:j + 1]),
                     [vg_r], [vn_r[j], ss_r])
                S.act(sg_ss[:, 4 + j:5 + j], sg_ss[:, j:j + 1], AF.Ln, [ss_r], [ss_r], scale=1.0 / D, bias=EPS)
                S.act(sg_ss[:, 4 + j:5 + j], sg_ss[:, 4 + j:5 + j], AF.Exp, [ss_r], [ss_r], scale=-0.5)
                S.stt(vn[:, j, :], vg[:], sg_ss[:, 4 + j:5 + j], gvn[:], ALU.mult, ALU.mult, [vg_r, ss_r, gvn_r], [vn_r[j]])
            if t + 1 < NTL:
                norm_a(t + 1, 7)
                norm_b(t + 1, 7, gi, xn[(t + 1) % 2], xn_r[(t + 1) % 2])
            for gg in range(8):
                b = gg % 6
                for j in range(4):
                    S.mm(ps[b][:, j * 128:(j + 1) * 128], vn[:, j, gg * 128:(gg + 1) * 128], wm[:, gg * 128:(gg + 1) * 128],
                         True, False, [vn_r[j], wm_r], [ps_r[b]])
                    S.mm(ps[b][:, j * 128:(j + 1) * 128], ones_bf[:], bsb[:, gg * 128:(gg + 1) * 128],
                         False, True, [ones_r, bsb_r], [ps_r[b]])
                S.tt(ub[:, gg, :], ub[:, gg, :], ps[b][:], ALU.mult, [u_r[gg], ps_r[b]], [u_r[gg]])
            out_proj(g, 8, ub, u_r, t, 1.0)

    bg.extend(prefetch_tasks(groups[0]))
    drain_bg()
    for li, lay in enumerate(layers):
        kind, idx = lay[0], lay[1]
        if li + 1 < len(layers):
            bg.extend(prefetch_tasks(groups[li + 1]))
        if kind == "ffn":
            ffn(groups[li], idx, 0.5)
        elif kind == "conv":
            conv(groups[li], idx, layers[li][2])
        elif kind == "sg":
            sgm(groups[li], idx)
        elif kind == "mla":
            mla(groups[li], idx)
        drain_bg()
        S.barrier()

    oTv = outT.rearrange("(c p) n -> p c n", p=128)
    outs = []
    for t in range(NTL):
        outs.append(S.dma(oTv[:, :, t * T:(t + 1) * T], xres[:, :, t * T:(t + 1) * T], reads=xr[t]))
    S.finish_wait("sp", outs)
    S.emit()
    return nc


def ffn_tiles(g, u, d):
    return np.concatenate([tile_pair(g, u), tile_rows(d, 2)], axis=0).reshape(33 * 128, SLOT)


def conv_tiles(w_in, w_out):
    wb, wc, wh = w_in[:, :D], w_in[:, D:2 * D], w_in[:, 2 * D:]
    return np.concatenate([tile_pair(wc, wh), tile_single(wb).reshape(4, 2, 128, D).transpose(0, 2, 1, 3).reshape(4, 128, SLOT),
                           tile_rows(w_out, 2)], axis=0).reshape(16 * 128, SLOT)


def sg_tiles(w_in, w_out):
    wu, wv = w_in[:, :D], w_in[:, D:]
    return np.concatenate([tile_single(wu).reshape(4, 2, 128, D).transpose(0, 2, 1, 3).reshape(4, 128, SLOT),
                           tile_rows(wv, 2), tile_rows(w_out, 2)], axis=0).reshape(12 * 128, SLOT)


def mla_tiles(w_a, w_o):
    wa = np.zeros((2, 128, SLOT), np.float32)
    t_ = w_a.reshape(8, 128, 416)
    for s_ in range(2):
        wa[s_, :, :4 * 416] = t_[4 * s_:4 * s_ + 4].transpose(1, 0, 2).reshape(128, 4 * 416)
    pad = np.zeros((2, 128, SLOT), np.float32)
    return np.concatenate([wa, tile_rows(w_o, 2), pad], axis=0).reshape(8 * 128, SLOT)


def mla_head_weights(w_uq, w_ukv, half):
    hq = np.zeros((128, 8, 2, 128), np.float32)
    hkv = np.zeros((128, 8, 128), np.float32)
    sw = [64 + (m + 16) % 32 for m in range(32)]
    for hl in range(8):
        h = half * 8 + hl
        wq = w_uq[:, h * 96:(h + 1) * 96]
        full = np.concatenate([wq, wq[:, sw]], axis=1)
        hq[:, hl] = full.reshape(2, 128, 128).transpose(1, 0, 2)
        hkv[:, hl] = w_ukv[:, h * 128:(h + 1) * 128]
    return np.ascontiguousarray(np.concatenate([hq.reshape(128, 2048), hkv.reshape(128, 1024)], axis=1))


def mla_consts(q_norm, kv_norm, q_gain, k_gain):
    sw = [64 + (m + 16) % 32 for m in range(32)]
    mg = np.zeros((128, 8), np.float32)
    mg[:96, 0] = q_gain
    mg[64:96, 1] = q_gain[sw]
    mg[:96, 2] = k_gain
    mg[64:96, 3] = k_gain[sw]
    mg[:, 4] = q_norm[:128]
    mg[:, 5] = q_norm[128:]
    mg[:, 6] = kv_norm
    mc = np.zeros((128, 320), np.float32)
    for m in range(32):
        mc[m, 64 + m] = 1.0
        mc[(m + 16) % 32, 96 + m] = 1.0
    mc[:, 128:256] = np.triu(np.ones((128, 128), np.float32))
    mc[64, 256:320] = 1.0
    inv_freq = (1.0 / (np.float32(10000.0) ** (np.arange(0, 32, 2, dtype=np.float32) / np.float32(32)))).astype(np.float32)
    rc = np.zeros((32, 2), np.float32)
    rc[:, 0] = np.concatenate([inv_freq, inv_freq])
    rc[:16, 1] = -1.0
    rc[16:, 1] = 1.0
    return mg, mc, rc


def shard_rows(w, c):
    r = w.shape[0] // NCORES
    return np.ascontiguousarray(w[c * r:(c + 1) * r])


LAYERS = [("ffn", 0), ("conv", 1, 0), ("ffn", 2),
          ("ffn", 3), ("mla", 4), ("ffn", 5),
          ("ffn", 6), ("sg", 7), ("ffn", 8),
          ("ffn", 9), ("conv", 10, 1), ("ffn", 11)]


def kernel(**inputs):
    f32 = lambda a: np.ascontiguousarray(np.asarray(a, dtype=np.float32))
    x = f32(inputs["x"])
    B, SEQ, _ = x.shape
    NT = SEQ // 2
    pos = np.ascontiguousarray(np.asarray(inputs["positions"]).astype(np.int32))
    ng = f32(inputs["norm_g"]).reshape(12, D)
    ngT = np.ascontiguousarray(ng.reshape(12, KC, 128).transpose(2, 0, 1).reshape(128, 12 * KC))
    fg, fu, fd = f32(inputs["ffn_gate"]), f32(inputs["ffn_up"]), f32(inputs["ffn_down"])
    cw_in, cw_k, cw_out = f32(inputs["conv_w_in"]), f32(inputs["conv_k"]), f32(inputs["conv_w_out"])
    tiles = []
    for lay in LAYERS:
        if lay[0] == "ffn":
            L, j = lay[1] // 3, (0 if lay[1] % 3 == 0 else 1)
            tiles.append(ffn_tiles(fg[L, j], fu[L, j], fd[L, j]))
        elif lay[0] == "conv":
            tiles.append(conv_tiles(cw_in[lay[2]], cw_out[lay[2]]))
        elif lay[0] == "mla":
            tiles.append(mla_tiles(f32(inputs["mla_w_a"])[0], f32(inputs["mla_w_o"])[0]))
        else:
            tiles.append(sg_tiles(f32(inputs["sg_w_in"])[0], f32(inputs["sg_w_out"])[0]))
    ckT = np.zeros((128, 48), np.float32)
    for l in range(2):
        ckT[:, l * 24:(l + 1) * 24] = cw_k[l].reshape(3, KC, 128).transpose(2, 0, 1).reshape(128, 24)
    mg, mc, rc = mla_consts(f32(inputs["mla_q_norm"])[0], f32(inputs["mla_kv_norm"])[0],
                            f32(inputs["mla_q_gain"])[0], f32(inputs["mla_k_gain"])[0])
    w_uq, w_ukv = f32(inputs["mla_w_uq"])[0], f32(inputs["mla_w_ukv"])[0]
    sgvn = np.ascontiguousarray(np.broadcast_to(f32(inputs["sg_v_norm"])[0][None], (128, D)))
    wsT = np.ascontiguousarray(f32(inputs["sg_w_s"])[0].transpose(2, 0, 1).reshape(128, D))
    mask01 = np.triu(np.ones((128, 128), np.float32))
    sgb = f32(inputs["sg_b"])[0].reshape(1, D)
    in_maps = []
    for c in range(NCORES):
        b, half = c // 2, c % 2
        m = dict(xT=np.ascontiguousarray(x[b, half * NT:(half + 1) * NT].T), ngT=ngT,
                 cmask=np.tile(np.array([[1 - half, half]], np.float32), (128, 1)), ckT=ckT,
                 pos32=np.ascontiguousarray(np.broadcast_to(pos[b][None], (32, SEQ))), ropec=rc, mlag=mg, mlac=mc,
                 mlahw=mla_head_weights(w_uq, w_ukv, half), sgvn=sgvn, wsT=wsT, mask01=mask01, sgb=sgb)
        for i, t_ in enumerate(tiles):
            m[f"w_g{i}"] = shard_rows(t_, c)
        in_maps.append(m)
    nc = build(NT, LAYERS)
    res = run_bass_kernel_spmd(nc, in_maps, core_ids=list(range(NCORES)))
    out = np.empty_like(x)
    for c in range(NCORES):
        b, half = c // 2, c % 2
        out[b, half * NT:(half + 1) * NT] = res.results[c]["outT"].T
    return out
```

```python
import numpy as np
import concourse.bass as bass
import concourse.mybir as mybir
from concourse.bass_utils import run_bass_kernel_spmd

F32 = mybir.dt.float32
BF16 = mybir.dt.bfloat16
I32 = mybir.dt.int32
AF = mybir.ActivationFunctionType
ALU = mybir.AluOpType

D = 1024
DFF = 2816
KC = 8
FC = 22
T = 512
EPS = 1e-6
NCORES = 8
SLOT = 2048


class Res:
    __slots__ = ("w", "r", "name")

    def __init__(self, name=""):
        self.w = None
        self.r = []
        self.name = name


class Op:
    __slots__ = ("eng", "fn", "waits", "sig", "val", "key", "seq", "inc")


class Sched:
    COMPUTE = ("pe", "act", "dve", "pool", "sp")

    def __init__(self, nc, n_dma_sems=12, same_engine_sync=True):
        self.nc = nc
        self.engobj = {"pe": nc.tensor, "act": nc.scalar, "dve": nc.vector,
                       "pool": nc.gpsimd, "sp": nc.sync}
        self.ops = []
        self.sems = {}
        self.seqc = {}
        self.seen = {e: {} for e in self.COMPUTE}
        self.same = same_engine_sync
        self.nds = n_dma_sems
        self.dnext = {"sp": 0, "pool": 0, "act": 0}
        self.dlast = {}
        self.lastop = {}
        for e in self.COMPUTE:
            self._key(e)

    def _key(self, key):
        if key not in self.sems:
            self.sems[key] = self.nc.alloc_semaphore("s_" + str(key).replace(" ", ""))
            self.seqc[key] = 0
        return key

    def _record(self, eng, fn, reads, writes, key, inc, extra_deps=()):
        op = Op()
        op.eng = eng
        op.fn = fn
        op.key = key
        op.inc = inc
        op.sig = (inc == 16) or key == "cc"
        op.val = None
        self.seqc[key] += 1
        op.seq = self.seqc[key]
        deps = set(extra_deps)
        for r in reads:
            if r.w is not None:
                deps.add(r.w)
        for w in writes:
            if w.w is not None:
                deps.add(w.w)
            deps.update(w.r)
        waits = {}
        for d in deps:
            if d.key == eng:
                if eng in ("pe", "sp") or not self.same:
                    continue
            if self.seen[eng].get(d.key, 0) >= d.seq:
                continue
            if d.key not in waits or waits[d.key].seq < d.seq:
                waits[d.key] = d
        for k, d in waits.items():
            self.seen[eng][k] = d.seq
            d.sig = True
        op.waits = list(waits.values())
        for r in reads:
            r.r.append(op)
        for w in writes:
            w.w = op
            w.r = []
        self.ops.append(op)
        if fn is not None:
            self.lastop[key] = op
        return op

    def barrier(self):
        lasts = list(self.lastop.values())
        for eng in self.COMPUTE:
            self._record(eng, None, [], [], eng, 1, extra_deps=lasts)

    def op(self, eng, fn, reads=(), writes=()):
        return self._record(eng, fn, reads, writes, eng, 1)

    def mm(self, out, lhsT, rhs, start, stop, reads, writes, **kw):
        return self.op("pe", lambda e: e.matmul(out=out, lhsT=lhsT, rhs=rhs, start=start, stop=stop, **kw), reads, writes)

    def act(self, out, in_, func, reads, writes, eng="act", **kw):
        return self.op(eng, lambda e: e.activation(out=out, in_=in_, func=func, **kw), reads, writes)

    def tt(self, out, in0, in1, op, reads, writes, eng="dve"):
        return self.op(eng, lambda e: e.tensor_tensor(out=out, in0=in0, in1=in1, op=op), reads, writes)

    def stt(self, out, in0, scalar, in1, op0, op1, reads, writes, eng="dve"):
        return self.op(eng, lambda e: e.scalar_tensor_tensor(out=out, in0=in0, scalar=scalar, in1=in1, op0=op0, op1=op1), reads, writes)

    def ts(self, out, in0, scalar1, scalar2, op0, op1, reads, writes, eng="dve"):
        return self.op(eng, lambda e: e.tensor_scalar(out=out, in0=in0, scalar1=scalar1, scalar2=scalar2, op0=op0, op1=op1), reads, writes)

    def copy(self, out, in_, reads, writes, eng="dve"):
        return self.op(eng, lambda e: e.tensor_copy(out=out, in_=in_), reads, writes)

    def memset(self, ap, val, writes, eng="dve"):
        return self.op(eng, lambda e: e.memset(ap, val), (), writes)

    def dma(self, out, in_, reads=(), writes=(), eng="sp"):
        k = self.dnext[eng]
        self.dnext[eng] = (k + 1) % self.nds
        key = self._key(("d", eng, k))
        extra = []
        if key in self.dlast:
            extra.append(self.dlast[key])
        op = self._record(eng, lambda e: e.dma_start(out=out, in_=in_), reads, writes, key, 16, extra)
        self.dlast[key] = op
        return op

    def collective(self, fn, reads=(), writes=()):
        key = self._key("cc")
        return self._record("pool", fn, reads, writes, key, 1)

    def finish_wait(self, eng, ops):
        self._record(eng, None, [], [], eng, 1, extra_deps=ops)

    def emit(self):
        cnt = {k: 0 for k in self.sems}
        for op in self.ops:
            e = self.engobj[op.eng]
            for d in op.waits:
                e.wait_ge(self.sems[d.key], d.val)
            if op.fn is None:
                continue
            inst = op.fn(e)
            if op.sig:
                cnt[op.key] += op.inc
                op.val = cnt[op.key]
                inst.then_inc(self.sems[op.key], op.inc)


class Ring:
    def __init__(self, S, nc, nslots, sb, name="ring"):
        self.S = S
        self.t = [sb(f"{name}{i}", [128, SLOT], BF16) for i in range(nslots)]
        self.r = [Res(f"{name}{i}") for i in range(nslots)]
        self.i = 0

    def load(self, src_ap, src_res, n=SLOT):
        i = self.i
        self.i = (i + 1) % len(self.t)
        self.S.dma(self.t[i][:, 0:n], src_ap, reads=[src_res], writes=[self.r[i]])
        return self.t[i], self.r[i]


def tile_pair(wa, wb):
    K = wa.shape[0] // 128
    M = wa.shape[1] // 128
    a = wa.reshape(K, 128, M, 128).transpose(2, 1, 0, 3)
    b = wb.reshape(K, 128, M, 128).transpose(2, 1, 0, 3)
    return np.ascontiguousarray(np.stack([a, b], axis=2).reshape(M, 128, 2 * K * 128))


def tile_single(w):
    K = w.shape[0] // 128
    M = w.shape[1] // 128
    return np.ascontiguousarray(w.reshape(K, 128, M, 128).transpose(2, 1, 0, 3).reshape(M, 128, K * 128))


def tile_rows(w, per=2):
    K = w.shape[0] // 128
    N = w.shape[1]
    return np.ascontiguousarray(w.reshape(K // per, per, 128, N).transpose(0, 2, 1, 3).reshape(K // per, 128, per * N))


class Ctx:
    pass


class WGroup:
    def __init__(self, nc, name, nslabs):
        self.name = name
        self.nslabs = nslabs
        self.rows = nslabs * 128 // NCORES
        self.win = nc.dram_tensor("w_" + name, [self.rows, SLOT], F32, kind="ExternalInput").ap()
        self.cin = nc.dram_tensor("ci_" + name, [self.rows, SLOT], BF16)
        self.cout = nc.dram_tensor("co_" + name, [nslabs * 128, SLOT], BF16)
        self.cin_r = Res("cin_" + name)
        self.cout_r = Res("cout_" + name)

    def slab(self, i):
        return self.cout.ap()[i * 128:(i + 1) * 128, :]


def build(NT, layers, dbg=False):
    nc = bass.Bass("TRN2", target_bir_lowering=False)
    S = Sched(nc)
    NTL = NT // T
    C = Ctx()
    C.nc, C.S, C.NT, C.NTL = nc, S, NT, NTL

    xT = nc.dram_tensor("xT", [D, NT], F32, kind="ExternalInput").ap()
    outT = nc.dram_tensor("outT", [D, NT], F32, kind="ExternalOutput").ap()
    ngT = nc.dram_tensor("ngT", [128, 12 * KC], F32, kind="ExternalInput").ap()
    NSLAB = {"ffn": 33, "conv": 16, "sg": 12, "mla": 8}
    groups = [WGroup(nc, f"g{i}", NSLAB[l[0]]) for i, l in enumerate(layers)]
    kinds = [l[0] for l in layers]
    cmaskT = nc.dram_tensor("cmask", [128, 2], F32, kind="ExternalInput").ap()
    if "conv" in kinds:
        ckT = nc.dram_tensor("ckT", [128, 2 * 24], F32, kind="ExternalInput").ap()
        zc_in = nc.dram_tensor("zc_in", [128, 16], F32)
        zc_out = nc.dram_tensor("zc_out", [256, 16], F32)
    if "mla" in kinds:
        SQ = 2 * NT
        posT = nc.dram_tensor("pos32", [32, SQ], I32, kind="ExternalInput").ap()
        ropec = nc.dram_tensor("ropec", [32, 2], F32, kind="ExternalInput").ap()
        mlag = nc.dram_tensor("mlag", [128, 8], F32, kind="ExternalInput").ap()
        mlac = nc.dram_tensor("mlac", [128, 256 + 64], F32, kind="ExternalInput").ap()
        mlahw = nc.dram_tensor("mlahw", [128, 3072], F32, kind="ExternalInput").ap()
        cs_tab = nc.dram_tensor("cs_tab", [64, SQ], F32)
        LC = min(NT, 1024)
        NLC = NT // LC
        lat_in = [nc.dram_tensor(f"lat_in{j}", [416, LC], BF16) for j in range(NLC)]
        lat_out = [nc.dram_tensor(f"lat_out{j}", [832, LC], BF16) for j in range(NLC)]
        o_in = [nc.dram_tensor(f"o_in{h}", [64, SQ], BF16) for h in range(8)]
        o_out = [nc.dram_tensor(f"o_out{h}", [128, SQ], BF16) for h in range(8)]
    if "sg" in kinds:
        sgvnT = nc.dram_tensor("sgvn", [128, D], F32, kind="ExternalInput").ap()
        wsT = nc.dram_tensor("wsT", [128, D], F32, kind="ExternalInput").ap()
        mask01 = nc.dram_tensor("mask01", [128, 128], F32, kind="ExternalInput").ap()
        sgbT = nc.dram_tensor("sgb", [1, D], F32, kind="ExternalInput").ap()

    base0 = nc.sbuf_base
    off = [(base0 + 63) // 64 * 64]

    def sb(name, shape, dt):
        n = 1
        for d_ in shape[1:]:
            n *= d_
        nb = n * (4 if dt in (F32, I32) else 2)
        t_ = nc.alloc_sbuf_tensor_at(name, shape, dt, offset=off[0])
        off[0] += (nb + 63) // 64 * 64
        return t_
    xres = sb("xres", [128, KC, NT], F32)
    xr = [[Res(f"x{t}_{c}") for c in range(KC)] for t in range(NTL)]
    ng = sb("ng", [128, 12 * KC], F32)
    ng_r = Res("ng")
    ones_bf = sb("ones_bf", [128, 128], BF16)
    ones_r = Res("ones")
    cmask = sb("cmask_sb", [128, 2], F32)
    cmask_r = Res("cmask")
    if "conv" in kinds:
        ck = sb("ck", [128, 48], F32)
        ck_r = Res("ck")
    if "mla" in kinds:
        mg = sb("mg", [128, 8], F32)
        mc16 = sb("mc16", [128, 256], BF16)
        sel65 = sb("sel65", [128, 64], F32)
        rpc = sb("rpc", [32, 2], F32)
        mconst_r = Res("mconst")
    xn_off = off[0]
    xn = [sb(f"xn{i}", [128, KC, T], BF16) for i in range(2)]
    xn_r = [Res(f"xn{i}") for i in range(2)]
    sq = [sb(f"sq{i}", [128, T], BF16) for i in range(2)]
    sq_r = [Res(f"sq{i}") for i in range(2)]
    rs = sb("rs", [128, T], F32)
    rs_r = Res("rs")
    sg_ss = sb("sg_ss", [128, 8], F32)
    ss_r = Res("ss")
    ring = Ring(S, nc, 5, sb)
    st32 = [sb(f"st32_{i}", [128, 512], F32) for i in range(2)]
    st16 = [sb(f"st16_{i}", [128, 512], BF16) for i in range(2)]
    st32_r = [Res(f"st32_{i}") for i in range(2)]
    st16_r = [Res(f"st16_{i}") for i in range(2)]
    arena0 = off[0]
    ARENA = 29696
    off[0] += ARENA
    assert off[0] <= nc.sbuf_top, (off[0], nc.sbuf_top)

    def arena_alloc():
        o = [arena0]

        def f(name, shape, dt):
            n = 1
            for d_ in shape[1:]:
                n *= d_
            nb = n * (4 if dt in (F32, I32) else 2)
            t_ = nc.alloc_sbuf_tensor_at(name, shape, dt, offset=o[0])
            o[0] += (nb + 63) // 64 * 64
            assert o[0] <= arena0 + ARENA, (name, o[0] - arena0)
            return t_
        return f
    ps = [nc.alloc_psum_tensor(f"ps{i}", [128, T], F32) for i in range(8)]
    ps_r = [Res(f"ps{i}") for i in range(8)]
    C.xres, C.xr, C.ng, C.ng_r, C.ones_bf, C.ones_r = xres, xr, ng, ng_r, ones_bf, ones_r
    C.xn, C.xn_r, C.sq, C.sq_r, C.rs, C.rs_r = xn, xn_r, sq, sq_r, rs, rs_r
    C.ring, C.ps, C.ps_r = ring, ps, ps_r

    S.memset(ones_bf[:], 1.0, [ones_r])
    S.dma(ng[:], ngT, writes=[ng_r])
    S.dma(cmask[:], cmaskT, writes=[cmask_r])
    if "conv" in kinds:
        S.dma(ck[:], ckT, writes=[ck_r])
    xTv = xT.rearrange("(c p) n -> p c n", p=128)
    for t in range(NTL):
        S.dma(xres[:, :, t * T:(t + 1) * T], xTv[:, :, t * T:(t + 1) * T], writes=xr[t])

    stc = [0]

    def prefetch_tasks(g):
        tasks = []
        for r0 in range(0, g.rows, 128):
            nr = min(128, g.rows - r0)
            for c0 in range(0, SLOT, 512):
                def task(r0=r0, nr=nr, c0=c0):
                    i = stc[0] % 2
                    stc[0] += 1
                    S.dma(st32[i][0:nr, :], g.win[r0:r0 + nr, c0:c0 + 512], writes=[st32_r[i]])
                    S.copy(st16[i][0:nr, :], st32[i][0:nr, :], [st32_r[i]], [st16_r[i]], eng="pool")
                    S.dma(g.cin.ap()[r0:r0 + nr, c0:c0 + 512], st16[i][0:nr, :], reads=[st16_r[i]], writes=[g.cin_r])
                tasks.append(task)

        def coll():
            S.collective(lambda e: e.collective_compute(
                "AllGather", ALU.bypass, replica_groups=[list(range(NCORES))],
                ins=[g.cin.ap().opt()], outs=[g.cout.ap().opt()]), reads=[g.cin_r], writes=[g.cout_r])
        tasks.append(coll)
        return tasks
    def ring_load_n(src, src_r, _unused):
        return ring.load(src, src_r)
    bg = []

    def tick(n=1):
        for _ in range(n):
            if bg:
                bg.pop(0)()

    def drain_bg():
        while bg:
            bg.pop(0)()

    def norm_a(t, nbank):
        for c in range(KC):
            S.act(sq[c % 2][:], xres[:, c, t * T:(t + 1) * T], AF.Square, [xr[t][c]], [sq_r[c % 2]])
            S.mm(ps[nbank][:], ones_bf[:], sq[c % 2][:], c == 0, c == KC - 1, [sq_r[c % 2], ones_r], [ps_r[nbank]])

    def norm_b(t, nbank, gi, dst, dst_r):
        S.act(rs[:], ps[nbank][:], AF.Ln, [ps_r[nbank]], [rs_r], scale=1.0 / D, bias=EPS)
        S.act(rs[:], rs[:], AF.Exp, [rs_r], [rs_r], scale=-0.5)
        for c in range(KC):
            S.stt(dst[:, c, :], xres[:, c, t * T:(t + 1) * T], ng[:, gi * KC + c:gi * KC + c + 1], rs[:],
                  ALU.mult, ALU.mult, [xr[t][c], ng_r, rs_r], [dst_r])

    def ffn(g, gi, coef):
        al = arena_alloc()
        hbuf = al("hbuf", [128, FC, T], BF16)
        h_r = [Res(f"h{m}") for m in range(FC)]
        sg = [al(f"sg{i}", [128, T], F32) for i in range(2)]
        sg_r = [Res(f"sg{i}") for i in range(2)]
        norm_a(0, 7)
        norm_b(0, 7, gi, xn[0], xn_r[0])
        for t in range(NTL):
            cur = xn[t % 2]
            cur_r = xn_r[t % 2]
            for m in range(FC):
                w, w_r = ring.load(g.slab(m), g.cout_r)
                gb, ub = (2 * m) % 4, (2 * m + 1) % 4
                for half, bank in ((0, gb), (1, ub)):
                    for k in range(KC):
                        o = (half * KC + k) * 128
                        S.mm(ps[bank][:], w[:, o:o + 128], cur[:, k, :], k == 0, k == KC - 1, [w_r, cur_r], [ps_r[bank]])
                S.act(sg[m % 2][:], ps[gb][:], AF.Silu, [ps_r[gb]], [sg_r[m % 2]])
                S.tt(hbuf[:, m, :], sg[m % 2][:], ps[ub][:], ALU.mult, [sg_r[m % 2], ps_r[ub]], [h_r[m]])
                tick()
                if t + 1 < NTL:
                    if m == 4:
                        norm_a(t + 1, 7)
                    if m == 8:
                        norm_b(t + 1, 7, gi, xn[(t + 1) % 2], xn_r[(t + 1) % 2])
            corder = [4, 5, 6, 7, 0, 1, 2, 3]
            for kk in range(FC // 2):
                w, w_r = ring.load(g.slab(FC + kk), g.cout_r)
                for i in range(2):
                    k = 2 * kk + i
                    for c in corder:
                        o = i * D + c * 128
                        S.mm(ps[c][:], w[:, o:o + 128], hbuf[:, k, :], k == 0, k == FC - 1, [w_r, h_r[k]], [ps_r[c]])
                tick()
            for c in corder:
                xs = xres[:, c, t * T:(t + 1) * T]
                S.stt(xs, ps[c][:], float(coef), xs, ALU.mult, ALU.add, [ps_r[c], xr[t][c]], [xr[t][c]])


    def out_proj(g, slab0, y, y_r, t, coef):
        corder = [4, 5, 6, 7, 0, 1, 2, 3]
        for kk in range(KC // 2):
            w, w_r = ring.load(g.slab(slab0 + kk), g.cout_r)
            for i in range(2):
                k = 2 * kk + i
                for c in corder:
                    o = i * D + c * 128
                    S.mm(ps[c][:], w[:, o:o + 128], y[:, k, :], k == 0, k == KC - 1, [w_r] + y_r, [ps_r[c]])
            tick()
        for c in corder:
            xs = xres[:, c, t * T:(t + 1) * T]
            S.stt(xs, ps[c][:], float(coef), xs, ALU.mult, ALU.add, [ps_r[c], xr[t][c]], [xr[t][c]])

    def conv(g, gi, ci):
        al = arena_alloc()
        zbuf = al("zbuf", [128, KC, T + 2], F32)
        z_r = [Res(f"z{c}") for c in range(KC)]
        ybuf = al("ybuf", [128, KC, T], BF16)
        y_r = [Res(f"y{c}") for c in range(KC)]
        cg = al("cg", [128, T], F32)
        cg_r = Res("cg")
        acc = al("acc", [128, T], F32)
        acc_r = Res("acc")
        xm = al("xm", [128, KC, 2], BF16)
        xm_r = Res("xm")
        small = al("small", [128, 64], F32)
        small_r = Res("small")
        kb = ci * 24
        for c in range(KC):
            S.act(sq[0][:, 2 * c:2 * c + 2], xres[:, c, NT - 2:NT], AF.Square, [xr[NTL - 1][c]], [sq_r[0]])
        for c in range(KC):
            S.mm(ps[7][:, 0:2], ones_bf[:], sq[0][:, 2 * c:2 * c + 2], c == 0, c == KC - 1, [sq_r[0], ones_r], [ps_r[7]])
        S.act(small[:, 0:2], ps[7][:, 0:2], AF.Ln, [ps_r[7]], [small_r], scale=1.0 / D, bias=EPS)
        S.act(small[:, 0:2], small[:, 0:2], AF.Exp, [small_r], [small_r], scale=-0.5)
        for c in range(KC):
            S.stt(xm[:, c, :], xres[:, c, NT - 2:NT], ng[:, gi * KC + c:gi * KC + c + 1], small[:, 0:2],
                  ALU.mult, ALU.mult, [xr[NTL - 1][c], ng_r, small_r], [xm_r])
        for c in range(KC):
            w, w_r = ring.load(g.slab(c), g.cout_r)
            for half, bank in ((0, 5), (1, 6)):
                for k in range(KC):
                    o = (half * KC + k) * 128
                    S.mm(ps[bank][:, 0:2], w[:, o:o + 128], xm[:, k, :], k == 0, k == KC - 1, [w_r, xm_r], [ps_r[bank]])
            S.act(cg[:, 0:2], ps[5][:, 0:2], AF.Copy, [ps_r[5]], [cg_r])
            S.tt(small[:, 16 + 2 * c:18 + 2 * c], cg[:, 0:2], ps[6][:, 0:2], ALU.mult, [cg_r, ps_r[6]], [small_r])
        zin_r, zout_r = Res("zin"), Res("zout")
        S.dma(zc_in.ap(), small[:, 16:32], reads=[small_r], writes=[zin_r])
        S.collective(lambda e: e.collective_compute(
            "AllGather", ALU.bypass, replica_groups=[[0, 1], [2, 3], [4, 5], [6, 7]],
            ins=[zc_in.ap().opt()], outs=[zc_out.ap().opt()]), reads=[zin_r], writes=[zout_r])
        S.dma(small[:, 32:48], zc_out.ap()[0:128, :], reads=[zout_r], writes=[small_r])
        S.ts(zbuf[:, :, 0:2], small[:, 32:48].rearrange("p (c j) -> p c j", j=2), cmask[:, 1:2], None, ALU.mult, ALU.bypass,
             [small_r, cmask_r], z_r)
        norm_a(0, 7)
        norm_b(0, 7, gi, xn[0], xn_r[0])
        for t in range(NTL):
            cur, cur_r = xn[t % 2], xn_r[t % 2]
            if t > 0:
                S.copy(zbuf[:, :, 0:2], zbuf[:, :, T:T + 2], z_r, z_r)
            for c in range(KC):
                w, w_r = ring.load(g.slab(c), g.cout_r)
                for half, bank in ((0, 0), (1, 1)):
                    for k in range(KC):
                        o = (half * KC + k) * 128
                        S.mm(ps[bank][:], w[:, o:o + 128], cur[:, k, :], k == 0, k == KC - 1, [w_r, cur_r], [ps_r[bank]])
                if c % 2 == 0:
                    w2, w2_r = ring.load(g.slab(8 + c // 2), g.cout_r)
                for k in range(KC):
                    o = ((c % 2) * KC + k) * 128
                    S.mm(ps[2][:], w2[:, o:o + 128], cur[:, k, :], k == 0, k == KC - 1, [w2_r, cur_r], [ps_r[2]])
                S.act(cg[:], ps[0][:], AF.Copy, [ps_r[0]], [cg_r])
                S.tt(zbuf[:, c, 2:T + 2], cg[:], ps[1][:], ALU.mult, [cg_r, ps_r[1]], [z_r[c]])
                S.ts(acc[:], zbuf[:, c, 2:T + 2], ck[:, kb + 16 + c:kb + 17 + c], None, ALU.mult, ALU.bypass, [z_r[c], ck_r], [acc_r])
                S.stt(acc[:], zbuf[:, c, 1:T + 1], ck[:, kb + 8 + c:kb + 9 + c], acc[:], ALU.mult, ALU.add, [z_r[c], ck_r, acc_r], [acc_r])
                S.stt(acc[:], zbuf[:, c, 0:T], ck[:, kb + c:kb + 1 + c], acc[:], ALU.mult, ALU.add, [z_r[c], ck_r, acc_r], [acc_r])
                S.tt(ybuf[:, c, :], acc[:], ps[2][:], ALU.mult, [acc_r, ps_r[2]], [y_r[c]])
                tick()
                if t + 1 < NTL:
                    if c == 2:
                        norm_a(t + 1, 7)
                    if c == 4:
                        norm_b(t + 1, 7, gi, xn[(t + 1) % 2], xn_r[(t + 1) % 2])
            out_proj(g, 12, ybuf, y_r, t, 1.0)


    def mla(g, gi):
        SQ = 2 * NT
        NS = SQ // T
        scale = 96.0 ** -0.5
        barrier = S.barrier
        al = arena_alloc()
        stg = al("stg", [128, 3072], F32)
        stg_r = Res("stg")
        posi = al("posi", [32, T], I32)
        posf = al("posf", [32, T], F32)
        kf = al("kf", [32, T], F32)
        ki = al("ki", [32, T], I32)
        rr = al("rr", [32, T], F32)
        cst = al("cst", [32, 2, T], F32)
        tr = Res("ropetmp")
        S.dma(mg[:], mlag, writes=[mconst_r])
        S.dma(rpc[:], ropec, writes=[mconst_r])
        S.dma(stg[:, 0:320], mlac, writes=[stg_r])
        S.copy(mc16[:], stg[:, 0:256], [stg_r], [mconst_r])
        S.copy(sel65[:], stg[:, 256:320], [stg_r], [mconst_r])
        barrier()
        cs_r = Res("cs_tab")
        TWO_PI = 2.0 * np.pi
        c1 = float(np.float32(6.28125))
        c2 = float(np.float32(TWO_PI - 6.28125))
        c3 = float(TWO_PI - 6.28125 - np.float64(np.float32(TWO_PI - 6.28125)))
        for st in range(NS):
            S.dma(posi[:], posT[:, st * T:(st + 1) * T], writes=[tr])
            S.copy(posf[:], posi[:], [tr], [tr])
            S.ts(posf[:], posf[:], rpc[:, 0:1], None, ALU.mult, ALU.bypass, [tr, mconst_r], [tr])
            S.ts(kf[:], posf[:], float(1.0 / TWO_PI), None, ALU.mult, ALU.bypass, [tr], [tr])
            S.copy(ki[:], kf[:], [tr], [tr])
            S.copy(kf[:], ki[:], [tr], [tr])
            S.stt(rr[:], kf[:], -c1, posf[:], ALU.mult, ALU.add, [tr], [tr])
            S.stt(rr[:], kf[:], -c2, rr[:], ALU.mult, ALU.add, [tr], [tr])
            S.stt(rr[:], kf[:], -c3, rr[:], ALU.mult, ALU.add, [tr], [tr])
            for ti, shift in ((0, float(np.pi / 2)), (1, 0.0)):
                y = cst[:, ti, :]
                S.ts(y, rr[:], shift, None, ALU.add, ALU.bypass, [tr], [tr])
                S.ts(kf[:], y, float(np.pi), None, ALU.is_gt, ALU.bypass, [tr], [tr])
                S.stt(y, kf[:], -TWO_PI, y, ALU.mult, ALU.add, [tr], [tr])
                S.ts(kf[:], y, -float(np.pi), None, ALU.is_lt, ALU.bypass, [tr], [tr])
                S.stt(y, kf[:], TWO_PI, y, ALU.mult, ALU.add, [tr], [tr])
            S.act(cst[:, 0, :], cst[:, 0, :], AF.Sin, [tr], [tr])
            S.act(cst[:, 1, :], cst[:, 1, :], AF.Sin, [tr], [tr])
            S.ts(cst[:, 1, :], cst[:, 1, :], rpc[:, 1:2], None, ALU.mult, ALU.bypass, [tr, mconst_r], [tr])
            S.dma(cs_tab.ap()[0:32, st * T:(st + 1) * T], cst[:, 0, :], reads=[tr], writes=[cs_r])
            S.dma(cs_tab.ap()[32:64, st * T:(st + 1) * T], cst[:, 1, :], reads=[tr], writes=[cs_r])
        barrier()

        al = arena_alloc()
        lq = al("lq", [128, 2, T], BF16)
        lkv = al("lkv", [128, T], BF16)
        lpe = al("lpe", [32, T], BF16)
        sqa = al("sqa", [128, T], BF16)
        sqb_ = al("sqb", [128, T], BF16)
        r1 = al("r1", [128, T], F32)
        m1r = Res("m1")
        lat_in_r = [Res(f"lat_in{j}") for j in range(NLC)]
        lat_out_r = [Res(f"lat_out{j}") for j in range(NLC)]
        TPC = LC // T
        norm_a(0, 7)
        norm_b(0, 7, gi, xn[0], xn_r[0])
        for t in range(NTL):
            cur, cur_r = xn[t % 2], xn_r[t % 2]
            ws = [ring_load_n(g.slab(i), g.cout_r, i + 2) for i in range(2)]
            for (c0, m, bank) in ((0, 128, 0), (128, 128, 1), (256, 128, 2), (384, 32, 3)):
                for k in range(KC):
                    w, w_r = ws[k // 4]
                    o = (k % 4) * 416 + c0
                    S.mm(ps[bank][0:m, :], w[:, o:o + m], cur[:, k, :], k == 0, k == KC - 1, [w_r, cur_r], [ps_r[bank]])
            S.act(sqa[:], ps[0][:], AF.Square, [ps_r[0]], [m1r])
            S.act(sqb_[:], ps[1][:], AF.Square, [ps_r[1]], [m1r])
            S.mm(ps[4][:], ones_bf[:], sqa[:], True, False, [m1r, ones_r], [ps_r[4]])
            S.mm(ps[4][:], ones_bf[:], sqb_[:], False, True, [m1r, ones_r], [ps_r[4]])
            S.act(r1[:], ps[4][:], AF.Ln, [ps_r[4]], [m1r], scale=1.0 / 256, bias=EPS)
            S.act(r1[:], r1[:], AF.Exp, [m1r], [m1r], scale=-0.5)
            for j in range(2):
                S.stt(lq[:, j, :], ps[j][:], mg[:, 4 + j:5 + j], r1[:], ALU.mult, ALU.mult, [ps_r[j], m1r, mconst_r], [m1r])
            lj, lc0 = t // TPC, (t % TPC) * T
            S.dma(lat_in[lj].ap()[0:256, lc0:lc0 + T].rearrange("(j p) n -> p j n", p=128), lq[:], reads=[m1r], writes=[lat_in_r[lj]])
            S.act(sqa[:], ps[2][:], AF.Square, [ps_r[2], m1r], [m1r])
            S.mm(ps[5][:], ones_bf[:], sqa[:], True, True, [m1r, ones_r], [ps_r[5]])
            S.act(r1[:], ps[5][:], AF.Ln, [ps_r[5], m1r], [m1r], scale=1.0 / 128, bias=EPS)
            S.act(r1[:], r1[:], AF.Exp, [m1r], [m1r], scale=-0.5)
            S.stt(lkv[:], ps[2][:], mg[:, 6:7], r1[:], ALU.mult, ALU.mult, [ps_r[2], m1r, mconst_r], [m1r])
            S.dma(lat_in[lj].ap()[256:384, lc0:lc0 + T], lkv[:], reads=[m1r], writes=[lat_in_r[lj]])
            S.copy(lpe[:], ps[3][0:32, :], [ps_r[3], m1r], [m1r])
            S.dma(lat_in[lj].ap()[384:416, lc0:lc0 + T], lpe[:], reads=[m1r], writes=[lat_in_r[lj]])
            if (t + 1) % TPC == 0:
                S.collective(lambda e, lj=lj: e.collective_compute(
                    "AllGather", ALU.bypass, replica_groups=[[0, 1], [2, 3], [4, 5], [6, 7]],
                    ins=[lat_in[lj].ap().opt()], outs=[lat_out[lj].ap().opt()]), reads=[lat_in_r[lj]], writes=[lat_out_r[lj]])
            tick(4)
            if t + 1 < NTL:
                norm_a(t + 1, 7)
                norm_b(t + 1, 7, gi, xn[(t + 1) % 2], xn_r[(t + 1) % 2])
        barrier()

        al = arena_alloc()
        stg = al("stg2", [128, 3072], F32)
        S.dma(stg[:], mlahw, writes=[stg_r])
        hq, hq_r = ring.t[0], ring.r[0]
        hkv, hkv_r = ring.t[1], ring.r[1]
        S.copy(hq[:], stg[:, 0:2048], [stg_r], [hq_r])
        S.copy(hkv[:, 0:1024], stg[:, 2048:3072], [stg_r], [hkv_r])
        barrier()
        al = arena_alloc()
        KT = al("KT", [128, SQ], BF16)
        Vh = al("Vh", [128, SQ // 128, 65], BF16)
        Oa = al("Oa", [128, T], F32)
        rl = al("rl", [128, T], F32)
        xo = [xn_off]

        def xa(name, shape, dt):
            n = 1
            for d_ in shape[1:]:
                n *= d_
            nb = n * (4 if dt in (F32, I32) else 2)
            t_ = nc.alloc_sbuf_tensor_at(name, shape, dt, offset=xo[0])
            xo[0] += nb
            assert xo[0] <= xn_off + 16384
            return t_
        pt = [xa(f"pt{i}", [128, T], BF16) for i in range(3)]
        qts = xa("qts", [128, T], BF16)
        sqh = sq[1]
        cqt = xa("cqt", [128, 2, T], BF16)
        ckvt = xa("ckvt", [128, T], BF16)
        kpet = xa("kpet", [128, T], BF16)
        Ct = xa("Ct", [128, T], F32)
        St = xa("St", [128, T], F32)
        tA = xa("tA", [128, T], F32)
        tB = xa("tB", [128, T], F32)
        on = sq[0]
        R = rs
        kt_r, vh_r, pt_r = Res("KT"), Res("Vh"), [Res("pt0"), Res("pt1"), Res("pt2")]
        qts_r, w_r_ = Res("qts"), Res("work")
        in_r = Res("mla_in")
        cs_in_r = Res("cs_in")
        oin_r = [Res(f"o_in{h}") for h in range(8)]
        oout_r = [Res(f"o_out{h}") for h in range(8)]
        onr = Res("on")
        S.memset(Vh[:, :, 64:65], 1.0, [vh_r])
        selI = mc16[0:32, 0:96]
        selSw = mc16[0:32, 96:128]
        m01 = mc16[:, 128:256]

        def lat_cols(st):
            r_, lt = st // NTL, st % NTL
            return r_ * 416, (lt % TPC) * T, lt // TPC

        def norm_rope(A, Bk, gcol, dst, dst_r, extra_scale):
            S.act(sqh[0:96, :], ps[A][0:96, :], AF.Square, [ps_r[A]], [w_r_])
            S.mm(ps[6][0:96, :], ones_bf[0:96, 0:96], sqh[0:96, :], True, True, [w_r_, ones_r], [ps_r[6]])
            S.act(R[0:96, :], ps[6][0:96, :], AF.Ln, [ps_r[6]], [w_r_], scale=1.0 / 96, bias=EPS)
            S.act(R[0:96, :], R[0:96, :], AF.Exp, [w_r_], [w_r_], scale=-0.5)
            S.stt(dst[0:64, :], ps[A][0:64, :], mg[0:64, gcol:gcol + 1], R[0:64, :], ALU.mult, ALU.mult,
                  [ps_r[A], w_r_, mconst_r], [dst_r])
            S.stt(tA[64:96, :], ps[A][64:96, :], mg[64:96, gcol:gcol + 1], Ct[64:96, :], ALU.mult, ALU.mult,
                  [ps_r[A], cs_in_r, mconst_r], [w_r_])
            S.stt(tB[64:96, :], ps[Bk][64:96, :], mg[64:96, gcol + 1:gcol + 2], St[64:96, :], ALU.mult, ALU.mult,
                  [ps_r[Bk], cs_in_r, mconst_r], [w_r_])
            S.tt(tA[64:96, :], tA[64:96, :], tB[64:96, :], ALU.add, [w_r_], [w_r_])
            S.tt(dst[64:96, :], tA[64:96, :], R[64:96, :], ALU.mult, [w_r_], [dst_r])

        def load_cs(st):
            S.dma(Ct[64:96, :], cs_tab.ap()[0:32, st * T:(st + 1) * T], reads=[cs_r], writes=[cs_in_r])
            S.dma(St[64:96, :], cs_tab.ap()[32:64, st * T:(st + 1) * T], reads=[cs_r], writes=[cs_in_r])

        for hl in range(8):
            wq = hq[:, hl * 256:(hl + 1) * 256]
            wk = hkv[:, hl * 128:hl * 128 + 64]
            wv = hkv[:, hl * 128 + 64:hl * 128 + 128]
            for st in range(NS):
                r0, c0, lj = lat_cols(st)
                S.dma(ckvt[:], lat_out[lj].ap()[r0 + 256:r0 + 384, c0:c0 + T], reads=[lat_out_r[lj]], writes=[in_r])
                S.dma(kpet[0:32, :], lat_out[lj].ap()[r0 + 384:r0 + 416, c0:c0 + T], reads=[lat_out_r[lj]], writes=[in_r])
                load_cs(st)
                S.mm(ps[0][0:64, :], wk, ckvt[:], True, True, [hkv_r, in_r], [ps_r[0]])
                S.mm(ps[0][64:96, :], selI[:, 64:96], kpet[0:32, :], True, True, [mconst_r, in_r], [ps_r[0]])
                S.mm(ps[1][64:96, :], selSw, kpet[0:32, :], True, True, [mconst_r, in_r], [ps_r[1]])
                norm_rope(0, 1, 2, KT[:, st * T:(st + 1) * T], kt_r, 1.0)
                for jb in range(4):
                    S.mm(ps[2][:, jb * 64:(jb + 1) * 64], ckvt[:, jb * 128:(jb + 1) * 128], wv, True, True, [in_r, hkv_r], [ps_r[2]])
                S.act(Vh[:, st * 4:(st + 1) * 4, 0:64], ps[2][:, 0:256].rearrange("p (j d) -> p j d", d=64), AF.Copy, [ps_r[2]], [vh_r])
                tick()
            for qt in range(NS):
                r0, c0, lj = lat_cols(qt)
                S.dma(cqt[:], lat_out[lj].ap()[r0:r0 + 256, c0:c0 + T].rearrange("(j p) n -> p j n", p=128), reads=[lat_out_r[lj]], writes=[in_r])
                load_cs(qt)
                for kc in range(2):
                    S.mm(ps[0][0:96, :], wq[:, kc * 128:kc * 128 + 96], cqt[:, kc, :], kc == 0, kc == 1, [hq_r, in_r], [ps_r[0]])
                for kc in range(2):
                    S.mm(ps[1][64:96, :], wq[:, kc * 128 + 96:kc * 128 + 128], cqt[:, kc, :], kc == 0, kc == 1, [hq_r, in_r], [ps_r[1]])
                norm_rope(0, 1, 0, qts, qts_r, 1.0)
                nkb = qt * 4 + 4

                def s_step(kb, qt=qt):
                    j = kb - qt * 4
                    cs0 = 0 if j <= 0 else j * 128
                    sb_ = 2 + kb % 2
                    S.mm(ps[sb_][:, cs0:T], KT[0:96, kb * 128:(kb + 1) * 128], qts[0:96, cs0:T], True, True, [kt_r, qts_r], [ps_r[sb_]])
                    p_, p_r = pt[kb % 3], pt_r[kb % 3]
                    S.act(p_[:, cs0:T], ps[sb_][:, cs0:T], AF.Exp, [ps_r[sb_]], [p_r], scale=scale)
                    if j >= 0:
                        S.tt(p_[:, cs0:cs0 + 128], p_[:, cs0:cs0 + 128], m01, ALU.mult, [p_r, mconst_r], [p_r], eng="pool")

                def pv_step(kb, qt=qt, nkb=nkb):
                    j = kb - qt * 4
                    cs0 = 0 if j <= 0 else j * 128
                    S.mm(ps[4][0:65, cs0:T], Vh[:, kb, :], pt[kb % 3][:, cs0:T], kb == 0, kb == nkb - 1, [vh_r, pt_r[kb % 3]], [ps_r[4]])
                s_step(0)
                for kb in range(nkb):
                    if kb + 1 < nkb:
                        s_step(kb + 1)
                    pv_step(kb)
                S.copy(Oa[0:65, :], ps[4][0:65, :], [ps_r[4]], [w_r_])
                S.mm(ps[5][0:64, :], sel65[0:65, :], Oa[0:65, :], True, True, [w_r_, mconst_r], [ps_r[5]])
                S.op("dve", lambda e: e.reciprocal(rl[0:64, :], ps[5][0:64, :]), [ps_r[5]], [w_r_])
                S.tt(on[0:64, :], Oa[0:64, :], rl[0:64, :], ALU.mult, [w_r_, onr], [onr])
                S.dma(o_in[hl].ap()[:, qt * T:(qt + 1) * T], on[0:64, :], reads=[onr], writes=[oin_r[hl]])
                tick()
            S.collective(lambda e, hl=hl: e.collective_compute(
                "AllGather", ALU.bypass, replica_groups=[[0, 1], [2, 3], [4, 5], [6, 7]],
                ins=[o_in[hl].ap().opt()], outs=[o_out[hl].ap().opt()]), reads=[oin_r[hl]], writes=[oout_r[hl]])
        barrier()

        al = arena_alloc()
        cand = [al(f"cand{i}", [128, KC, T], BF16) for i in range(2)]
        yb = al("yb", [128, KC, T], BF16)
        tmp = al("tmp3", [128, T], F32)
        cand_r, y_r3, tmp_r = Res("cand"), [Res(f"y3_{c}") for c in range(KC)], Res("tmp3")
        for t in range(NTL):
            for i in range(2):
                for gh in range(16):
                    r_, hl_ = gh // 8, gh % 8
                    S.dma(cand[i][(gh % 2) * 64:(gh % 2) * 64 + 64, gh // 2, :],
                          o_out[hl_].ap()[r_ * 64:(r_ + 1) * 64, i * NT + t * T:i * NT + (t + 1) * T],
                          reads=[oout_r[hl_]], writes=[cand_r])
            for c in range(KC):
                S.ts(tmp[:], cand[0][:, c, :], cmask[:, 0:1], None, ALU.mult, ALU.bypass, [cand_r, cmask_r], [tmp_r])
                S.stt(yb[:, c, :], cand[1][:, c, :], cmask[:, 1:2], tmp[:], ALU.mult, ALU.add, [cand_r, cmask_r, tmp_r], [y_r3[c]])
            out_proj(g, 2, yb, y_r3, t, 1.0)
        barrier()

    def sgm(g, gi):
        al = arena_alloc()
        ub = al("ub", [128, KC, T], BF16)
        u_r = [Res(f"u{c}") for c in range(KC)]
        vn = al("vn", [128, 4, D], BF16)
        vn_r = [Res(f"vn{j}") for j in range(4)]
        vg = al("vg", [128, D], F32)
        vg_r = Res("vg")
        wm = al("wm", [128, D], BF16)
        wm_r = Res("wm")
        gvn = al("gvn", [128, D], F32)
        gvn_r = Res("gvn")
        bsb = al("bsb", [128, D], BF16)
        bsb_r = Res("bsb")
        S.dma(gvn[:], sgvnT, writes=[gvn_r])
        S.dma(vg[:], wsT, writes=[vg_r])
        S.dma(rs[:, 0:128], mask01, writes=[rs_r])
        for gg in range(8):
            S.tt(wm[:, gg * 128:(gg + 1) * 128], vg[:, gg * 128:(gg + 1) * 128], rs[:, 0:128], ALU.mult, [vg_r, rs_r], [wm_r])
        S.barrier()
        S.dma(vg[0:1, :], sgbT, writes=[vg_r])
        S.memset(bsb[:], 0.0, [bsb_r])
        S.copy(bsb[0:1, :], vg[0:1, :], [vg_r, bsb_r], [bsb_r])
        S.barrier()
        norm_a(0, 7)
        norm_b(0, 7, gi, xn[0], xn_r[0])
        for t in range(NTL):
            cur, cur_r = xn[t % 2], xn_r[t % 2]
            for cc in range(4):
                w, w_r = ring.load(g.slab(cc), g.cout_r)
                for j in range(2):
                    c = 2 * cc + j
                    for k in range(KC):
                        o = (j * KC + k) * 128
                        S.mm(ps[c % 4][:], w[:, o:o + 128], cur[:, k, :], k == 0, k == KC - 1, [w_r, cur_r], [ps_r[c % 4]])
                    S.act(ub[:, c, :], ps[c % 4][:], AF.Gelu, [ps_r[c % 4]], [u_r[c]])
                tick()
            for kk in range(4):
                w, w_r = ring.load(g.slab(4 + kk), g.cout_r)
                for i in range(2):
                    k = 2 * kk + i
                    for j in range(4):
                        for hf in range(2):
                            S.mm(ps[j * 2 + hf][:], cur[:, k, j * 128:(j + 1) * 128], w[:, i * D + hf * 512:i * D + hf * 512 + 512],
                                 k == 0, k == KC - 1, [w_r, cur_r], [ps_r[j * 2 + hf]])
                tick()
            for j in range(4):
                for hf in range(2):
                    S.act(vg[:, hf * 512:(hf + 1) * 512], ps[j * 2 + hf][:], AF.Gelu, [ps_r[j * 2 + hf]], [vg_r])
                S.op("act", lambda e, j=j: e.activation(out=vn[:, j, :], in_=vg[:], func=AF.Square, accum_out=sg_ss[:, j:j + 1]),
                     [vg_r], [vn_r[j], ss_r])
                S.act(sg_ss[:, 4 + j:5 + j], sg_ss[:, j:j + 1], AF.Ln, [ss_r], [ss_r], scale=1.0 / D, bias=EPS)
                S.act(sg_ss[:, 4 + j:5 + j], sg_ss[:, 4 + j:5 + j], AF.Exp, [ss_r], [ss_r], scale=-0.5)
                S.stt(vn[:, j, :], vg[:], sg_ss[:, 4 + j:5 + j], gvn[:], ALU.mult, ALU.mult, [vg_r, ss_r, gvn_r], [vn_r[j]])
            if t + 1 < NTL:
                norm_a(t + 1, 7)
                norm_b(t + 1, 7, gi, xn[(t + 1) % 2], xn_r[(t + 1) % 2])
            for gg in range(8):
                b = gg % 6
                for j in range(4):
                    S.mm(ps[b][:, j * 128:(j + 1) * 128], vn[:, j, gg * 128:(gg + 1) * 128], wm[:, gg * 128:(gg + 1) * 128],
                         True, False, [vn_r[j], wm_r], [ps_r[b]])
                    S.mm(ps[b][:, j * 128:(j + 1) * 128], ones_bf[:], bsb[:, gg * 128:(gg + 1) * 128],
                         False, True, [ones_r, bsb_r], [ps_r[b]])
                S.tt(ub[:, gg, :], ub[:, gg, :], ps[b][:], ALU.mult, [u_r[gg], ps_r[b]], [u_r[gg]])
            out_proj(g, 8, ub, u_r, t, 1.0)

    bg.extend(prefetch_tasks(groups[0]))
    drain_bg()
    for li, lay in enumerate(layers):
        kind, idx = lay[0], lay[1]
        if li + 1 < len(layers):
            bg.extend(prefetch_tasks(groups[li + 1]))
        if kind == "ffn":
            ffn(groups[li], idx, 0.5)
        elif kind == "conv":
            conv(groups[li], idx, layers[li][2])
        elif kind == "sg":
            sgm(groups[li], idx)
        elif kind == "mla":
            mla(groups[li], idx)
        drain_bg()
        S.barrier()

    oTv = outT.rearrange("(c p) n -> p c n", p=128)
    outs = []
    for t in range(NTL):
        outs.append(S.dma(oTv[:, :, t * T:(t + 1) * T], xres[:, :, t * T:(t + 1) * T], reads=xr[t]))
    S.finish_wait("sp", outs)
    S.emit()
    return nc


def ffn_tiles(g, u, d):
    return np.concatenate([tile_pair(g, u), tile_rows(d, 2)], axis=0).reshape(33 * 128, SLOT)


def conv_tiles(w_in, w_out):
    wb, wc, wh = w_in[:, :D], w_in[:, D:2 * D], w_in[:, 2 * D:]
    return np.concatenate([tile_pair(wc, wh), tile_single(wb).reshape(4, 2, 128, D).transpose(0, 2, 1, 3).reshape(4, 128, SLOT),
                           tile_rows(w_out, 2)], axis=0).reshape(16 * 128, SLOT)


def sg_tiles(w_in, w_out):
    wu, wv = w_in[:, :D], w_in[:, D:]
    return np.concatenate([tile_single(wu).reshape(4, 2, 128, D).transpose(0, 2, 1, 3).reshape(4, 128, SLOT),
                           tile_rows(wv, 2), tile_rows(w_out, 2)], axis=0).reshape(12 * 128, SLOT)


def mla_tiles(w_a, w_o):
    wa = np.zeros((2, 128, SLOT), np.float32)
    t_ = w_a.reshape(8, 128, 416)
    for s_ in range(2):
        wa[s_, :, :4 * 416] = t_[4 * s_:4 * s_ + 4].transpose(1, 0, 2).reshape(128, 4 * 416)
    pad = np.zeros((2, 128, SLOT), np.float32)
    return np.concatenate([wa, tile_rows(w_o, 2), pad], axis=0).reshape(8 * 128, SLOT)


def mla_head_weights(w_uq, w_ukv, half):
    hq = np.zeros((128, 8, 2, 128), np.float32)
    hkv = np.zeros((128, 8, 128), np.float32)
    sw = [64 + (m + 16) % 32 for m in range(32)]
    for hl in range(8):
        h = half * 8 + hl
        wq = w_uq[:, h * 96:(h + 1) * 96]
        full = np.concatenate([wq, wq[:, sw]], axis=1)
        hq[:, hl] = full.reshape(2, 128, 128).transpose(1, 0, 2)
        hkv[:, hl] = w_ukv[:, h * 128:(h + 1) * 128]
    return np.ascontiguousarray(np.concatenate([hq.reshape(128, 2048), hkv.reshape(128, 1024)], axis=1))


def mla_consts(q_norm, kv_norm, q_gain, k_gain):
    sw = [64 + (m + 16) % 32 for m in range(32)]
    mg = np.zeros((128, 8), np.float32)
    mg[:96, 0] = q_gain
    mg[64:96, 1] = q_gain[sw]
    mg[:96, 2] = k_gain
    mg[64:96, 3] = k_gain[sw]
    mg[:, 4] = q_norm[:128]
    mg[:, 5] = q_norm[128:]
    mg[:, 6] = kv_norm
    mc = np.zeros((128, 320), np.float32)
    for m in range(32):
        mc[m, 64 + m] = 1.0
        mc[(m + 16) % 32, 96 + m] = 1.0
    mc[:, 128:256] = np.triu(np.ones((128, 128), np.float32))
    mc[64, 256:320] = 1.0
    inv_freq = (1.0 / (np.float32(10000.0) ** (np.arange(0, 32, 2, dtype=np.float32) / np.float32(32)))).astype(np.float32)
    rc = np.zeros((32, 2), np.float32)
    rc[:, 0] = np.concatenate([inv_freq, inv_freq])
    rc[:16, 1] = -1.0
    rc[16:, 1] = 1.0
    return mg, mc, rc


def shard_rows(w, c):
    r = w.shape[0] // NCORES
    return np.ascontiguousarray(w[c * r:(c + 1) * r])


LAYERS = [("ffn", 0), ("conv", 1, 0), ("ffn", 2),
          ("ffn", 3), ("mla", 4), ("ffn", 5),
          ("ffn", 6), ("sg", 7), ("ffn", 8),
          ("ffn", 9), ("conv", 10, 1), ("ffn", 11)]


def kernel(**inputs):
    f32 = lambda a: np.ascontiguousarray(np.asarray(a, dtype=np.float32))
    x = f32(inputs["x"])
    B, SEQ, _ = x.shape
    NT = SEQ // 2
    pos = np.ascontiguousarray(np.asarray(inputs["positions"]).astype(np.int32))
    ng = f32(inputs["norm_g"]).reshape(12, D)
    ngT = np.ascontiguousarray(ng.reshape(12, KC, 128).transpose(2, 0, 1).reshape(128, 12 * KC))
    fg, fu, fd = f32(inputs["ffn_gate"]), f32(inputs["ffn_up"]), f32(inputs["ffn_down"])
    cw_in, cw_k, cw_out = f32(inputs["conv_w_in"]), f32(inputs["conv_k"]), f32(inputs["conv_w_out"])
    tiles = []
    for lay in LAYERS:
        if lay[0] == "ffn":
            L, j = lay[1] // 3, (0 if lay[1] % 3 == 0 else 1)
            tiles.append(ffn_tiles(fg[L, j], fu[L, j], fd[L, j]))
        elif lay[0] == "conv":
            tiles.append(conv_tiles(cw_in[lay[2]], cw_out[lay[2]]))
        elif lay[0] == "mla":
            tiles.append(mla_tiles(f32(inputs["mla_w_a"])[0], f32(inputs["mla_w_o"])[0]))
        else:
            tiles.append(sg_tiles(f32(inputs["sg_w_in"])[0], f32(inputs["sg_w_out"])[0]))
    ckT = np.zeros((128, 48), np.float32)
    for l in range(2):
        ckT[:, l * 24:(l + 1) * 24] = cw_k[l].reshape(3, KC, 128).transpose(2, 0, 1).reshape(128, 24)
    mg, mc, rc = mla_consts(f32(inputs["mla_q_norm"])[0], f32(inputs["mla_kv_norm"])[0],
                            f32(inputs["mla_q_gain"])[0], f32(inputs["mla_k_gain"])[0])
    w_uq, w_ukv = f32(inputs["mla_w_uq"])[0], f32(inputs["mla_w_ukv"])[0]
    sgvn = np.ascontiguousarray(np.broadcast_to(f32(inputs["sg_v_norm"])[0][None], (128, D)))
    wsT = np.ascontiguousarray(f32(inputs["sg_w_s"])[0].transpose(2, 0, 1).reshape(128, D))
    mask01 = np.triu(np.ones((128, 128), np.float32))
    sgb = f32(inputs["sg_b"])[0].reshape(1, D)
    in_maps = []
    for c in range(NCORES):
        b, half = c // 2, c % 2
        m = dict(xT=np.ascontiguousarray(x[b, half * NT:(half + 1) * NT].T), ngT=ngT,
                 cmask=np.tile(np.array([[1 - half, half]], np.float32), (128, 1)), ckT=ckT,
                 pos32=np.ascontiguousarray(np.broadcast_to(pos[b][None], (32, SEQ))), ropec=rc, mlag=mg, mlac=mc,
                 mlahw=mla_head_weights(w_uq, w_ukv, half), sgvn=sgvn, wsT=wsT, mask01=mask01, sgb=sgb)
        for i, t_ in enumerate(tiles):
            m[f"w_g{i}"] = shard_rows(t_, c)
        in_maps.append(m)
    nc = build(NT, LAYERS)
    res = run_bass_kernel_spmd(nc, in_maps, core_ids=list(range(NCORES)))
    out = np.empty_like(x)
    for c in range(NCORES):
        b, half = c // 2, c % 2
        out[b, half * NT:(half + 1) * NT] = res.results[c]["outT"].T
    return out
```

```python
import numpy as np
import concourse.bass as bass
import concourse.mybir as mybir
from concourse.bass_utils import run_bass_kernel_spmd

F32 = mybir.dt.float32
BF16 = mybir.dt.bfloat16
I32 = mybir.dt.int32
AF = mybir.ActivationFunctionType
ALU = mybir.AluOpType

D = 1024
DFF = 2816
KC = 8
FC = 22
T = 512
EPS = 1e-6
import os as _os
KFLIGHT = int(_os.environ.get('KFLIGHT', '2'))
QINTER = int(_os.environ.get('QINTER', '1'))
NCORES = 8
SLOT = 2048


class Res:
    __slots__ = ("w", "r", "name", "ps")

    def __init__(self, name="", ps=False):
        self.w = None
        self.r = []
        self.name = name
        self.ps = ps


class Op:
    __slots__ = ("eng", "fn", "waits", "sig", "val", "key", "seq", "inc")


class Sched:
    COMPUTE = ("pe", "act", "dve", "pool", "sp")

    def __init__(self, nc, n_dma_sems=12, same_engine_sync=True):
        self.nc = nc
        self.engobj = {"pe": nc.tensor, "act": nc.scalar, "dve": nc.vector,
                       "pool": nc.gpsimd, "sp": nc.sync}
        self.ops = []
        self.sems = {}
        self.seqc = {}
        self.seen = {e: {} for e in self.COMPUTE}
        self.same = same_engine_sync
        self.nds = n_dma_sems
        self.dnext = {"sp": 0, "pool": 0, "act": 0}
        self.dlast = {}
        self.lastop = {}
        for e in self.COMPUTE:
            self._key(e)

    def _key(self, key):
        if key not in self.sems:
            self.sems[key] = self.nc.alloc_semaphore("s_" + str(key).replace(" ", ""))
            self.seqc[key] = 0
        return key

    def _record(self, eng, fn, reads, writes, key, inc, extra_deps=()):
        op = Op()
        op.eng = eng
        op.fn = fn
        op.key = key
        op.inc = inc
        op.sig = (inc == 16) or key == "cc"
        op.val = None
        self.seqc[key] += 1
        op.seq = self.seqc[key]
        deps = set(extra_deps)
        for r in reads:
            if r.w is not None:
                deps.add(r.w)
            if r.ps:
                deps.update(x for x in r.r if x.eng != eng)
        for w in writes:
            if w.w is not None:
                deps.add(w.w)
            deps.update(w.r)
        waits = {}
        for d in deps:
            if d.key == eng:
                if eng in ("pe", "sp") or not self.same:
                    continue
            if self.seen[eng].get(d.key, 0) >= d.seq:
                continue
            if d.key not in waits or waits[d.key].seq < d.seq:
                waits[d.key] = d
        for k, d in waits.items():
            self.seen[eng][k] = d.seq
            d.sig = True
        op.waits = list(waits.values())
        for r in reads:
            r.r.append(op)
        for w in writes:
            w.w = op
            w.r = []
        self.ops.append(op)
        if fn is not None:
            self.lastop[key] = op
        return op

    def barrier(self):
        lasts = list(self.lastop.values())
        for eng in self.COMPUTE:
            self._record(eng, None, [], [], eng, 1, extra_deps=lasts)

    def op(self, eng, fn, reads=(), writes=()):
        return self._record(eng, fn, reads, writes, eng, 1)

    def mm(self, out, lhsT, rhs, start, stop, reads, writes, **kw):
        return self.op("pe", lambda e: e.matmul(out=out, lhsT=lhsT, rhs=rhs, start=start, stop=stop, **kw), reads, writes)

    def act(self, out, in_, func, reads, writes, eng="act", **kw):
        return self.op(eng, lambda e: e.activation(out=out, in_=in_, func=func, **kw), reads, writes)

    def tt(self, out, in0, in1, op, reads, writes, eng="dve"):
        return self.op(eng, lambda e: e.tensor_tensor(out=out, in0=in0, in1=in1, op=op), reads, writes)

    def stt(self, out, in0, scalar, in1, op0, op1, reads, writes, eng="dve"):
        return self.op(eng, lambda e: e.scalar_tensor_tensor(out=out, in0=in0, scalar=scalar, in1=in1, op0=op0, op1=op1), reads, writes)

    def ts(self, out, in0, scalar1, scalar2, op0, op1, reads, writes, eng="dve"):
        return self.op(eng, lambda e: e.tensor_scalar(out=out, in0=in0, scalar1=scalar1, scalar2=scalar2, op0=op0, op1=op1), reads, writes)

    def copy(self, out, in_, reads, writes, eng="dve"):
        return self.op(eng, lambda e: e.tensor_copy(out=out, in_=in_), reads, writes)

    def memset(self, ap, val, writes, eng="dve"):
        return self.op(eng, lambda e: e.memset(ap, val), (), writes)

    def dma(self, out, in_, reads=(), writes=(), eng="sp"):
        k = self.dnext[eng]
        self.dnext[eng] = (k + 1) % self.nds
        key = self._key(("d", eng, k))
        extra = []
        if key in self.dlast:
            extra.append(self.dlast[key])
        op = self._record(eng, lambda e: e.dma_start(out=out, in_=in_), reads, writes, key, 16, extra)
        self.dlast[key] = op
        return op

    def collective(self, fn, reads=(), writes=()):
        key = self._key("cc")
        return self._record("pool", fn, reads, writes, key, 1)

    def finish_wait(self, eng, ops):
        self._record(eng, None, [], [], eng, 1, extra_deps=ops)

    def emit(self):
        cnt = {k: 0 for k in self.sems}
        for op in self.ops:
            e = self.engobj[op.eng]
            for d in op.waits:
                e.wait_ge(self.sems[d.key], d.val)
            if op.fn is None:
                continue
            inst = op.fn(e)
            if op.sig:
                cnt[op.key] += op.inc
                op.val = cnt[op.key]
                inst.then_inc(self.sems[op.key], op.inc)


class Ring:
    def __init__(self, S, nc, nslots, sb, name="ring"):
        self.S = S
        self.t = [sb(f"{name}{i}", [128, SLOT], BF16) for i in range(nslots)]
        self.r = [Res(f"{name}{i}") for i in range(nslots)]
        self.i = 0

    def load(self, src_ap, src_res, n=SLOT):
        i = self.i
        self.i = (i + 1) % len(self.t)
        self.S.dma(self.t[i][:, 0:n], src_ap, reads=[src_res], writes=[self.r[i]])
        return self.t[i], self.r[i]


def tile_pair(wa, wb):
    K = wa.shape[0] // 128
    M = wa.shape[1] // 128
    a = wa.reshape(K, 128, M, 128).transpose(2, 1, 0, 3)
    b = wb.reshape(K, 128, M, 128).transpose(2, 1, 0, 3)
    return np.ascontiguousarray(np.stack([a, b], axis=2).reshape(M, 128, 2 * K * 128))


def tile_single(w):
    K = w.shape[0] // 128
    M = w.shape[1] // 128
    return np.ascontiguousarray(w.reshape(K, 128, M, 128).transpose(2, 1, 0, 3).reshape(M, 128, K * 128))


def tile_rows(w, per=2):
    K = w.shape[0] // 128
    N = w.shape[1]
    return np.ascontiguousarray(w.reshape(K // per, per, 128, N).transpose(0, 2, 1, 3).reshape(K // per, 128, per * N))


class Ctx:
    pass


class WGroup:
    def __init__(self, nc, name, nslabs):
        self.name = name
        self.nslabs = nslabs
        self.rows = nslabs * 128 // NCORES
        self.win = nc.dram_tensor("w_" + name, [self.rows, SLOT], F32, kind="ExternalInput").ap()
        self.cin = nc.dram_tensor("ci_" + name, [self.rows, SLOT], BF16)
        self.cout = nc.dram_tensor("co_" + name, [nslabs * 128, SLOT], BF16)
        self.cin_r = Res("cin_" + name)
        self.cout_r = Res("cout_" + name)

    def slab(self, i):
        return self.cout.ap()[i * 128:(i + 1) * 128, :]


def build(NT, layers, dbg=False):
    nc = bass.Bass("TRN2", target_bir_lowering=False)
    S = Sched(nc)
    NTL = NT // T
    C = Ctx()
    C.nc, C.S, C.NT, C.NTL = nc, S, NT, NTL

    xT = nc.dram_tensor("xT", [D, NT], F32, kind="ExternalInput").ap()
    outT = nc.dram_tensor("outT", [D, NT], F32, kind="ExternalOutput").ap()
    ngT = nc.dram_tensor("ngT", [128, 12 * KC], F32, kind="ExternalInput").ap()
    NSLAB = {"ffn": 33, "conv": 16, "sg": 12, "mla": 8}
    groups = [WGroup(nc, f"g{i}", NSLAB[l[0]]) for i, l in enumerate(layers)]
    kinds = [l[0] for l in layers]
    cmaskT = nc.dram_tensor("cmask", [128, 2], F32, kind="ExternalInput").ap()
    if "conv" in kinds:
        ckT = nc.dram_tensor("ckT", [128, 2 * 24], F32, kind="ExternalInput").ap()
        zc_in = nc.dram_tensor("zc_in", [128, 16], F32)
        zc_out = nc.dram_tensor("zc_out", [256, 16], F32)
    if "mla" in kinds:
        SQ = 2 * NT
        posT = nc.dram_tensor("pos32", [32, SQ], I32, kind="ExternalInput").ap()
        ropec = nc.dram_tensor("ropec", [32, 2], F32, kind="ExternalInput").ap()
        mlag = nc.dram_tensor("mlag", [128, 8], F32, kind="ExternalInput").ap()
        mlac = nc.dram_tensor("mlac", [128, 256 + 64], F32, kind="ExternalInput").ap()
        mlahw = nc.dram_tensor("mlahw", [128, 3072], F32, kind="ExternalInput").ap()
        cs_tab = nc.dram_tensor("cs_tab", [64, SQ], BF16)
        LC = min(NT, 1024)
        NLC = NT // LC
        lat_in = [nc.dram_tensor(f"lat_in{j}", [416, LC], BF16) for j in range(NLC)]
        lat_out = [nc.dram_tensor(f"lat_out{j}", [832, LC], BF16) for j in range(NLC)]
        o_in = [nc.dram_tensor(f"o_in{h}", [64, SQ], BF16) for h in range(8)]
        o_out = [nc.dram_tensor(f"o_out{h}", [128, SQ], BF16) for h in range(8)]
    if "sg" in kinds:
        sgvnT = nc.dram_tensor("sgvn", [128, D], F32, kind="ExternalInput").ap()
        wsT = nc.dram_tensor("wsT", [128, D], F32, kind="ExternalInput").ap()
        mask01 = nc.dram_tensor("mask01", [128, 128], F32, kind="ExternalInput").ap()
        sgbT = nc.dram_tensor("sgb", [1, D], F32, kind="ExternalInput").ap()

    base0 = nc.sbuf_base
    off = [(base0 + 63) // 64 * 64]

    def sb(name, shape, dt):
        n = 1
        for d_ in shape[1:]:
            n *= d_
        nb = n * (4 if dt in (F32, I32) else 2)
        t_ = nc.alloc_sbuf_tensor_at(name, shape, dt, offset=off[0])
        off[0] += (nb + 63) // 64 * 64
        return t_
    xres = sb("xres", [128, KC, NT], F32)
    xr = [[Res(f"x{t}_{c}") for c in range(KC)] for t in range(NTL)]
    ng = sb("ng", [128, 12 * KC], F32)
    ng_r = Res("ng")
    ones_bf = sb("ones_bf", [128, 128], BF16)
    ones_r = Res("ones")
    cmask = sb("cmask_sb", [128, 2], F32)
    cmask_r = Res("cmask")
    if "conv" in kinds:
        ck = sb("ck", [128, 48], F32)
        ck_r = Res("ck")
    if "mla" in kinds:
        mg = sb("mg", [128, 8], F32)
        mc16 = sb("mc16", [128, 256], BF16)
        sel65 = sb("sel65", [128, 64], F32)
        rpc = sb("rpc", [32, 2], F32)
        mconst_r = Res("mconst")
    xn_off = off[0]
    xn = [sb(f"xn{i}", [128, KC, T], BF16) for i in range(2)]
    xn_r = [Res(f"xn{i}") for i in range(2)]
    sq = [sb(f"sq{i}", [128, T], BF16) for i in range(2)]
    sq_r = [Res(f"sq{i}") for i in range(2)]
    rs = sb("rs", [128, T], F32)
    rs_r = Res("rs")
    sg_ss = sb("sg_ss", [128, 8], F32)
    ss_r = Res("ss")
    ring_off = off[0]
    ring = Ring(S, nc, 5, sb)
    st32 = [sb(f"st32_{i}", [128, 512], F32) for i in range(2)]
    st16 = [sb(f"st16_{i}", [128, 512], BF16) for i in range(2)]
    st32_r = [Res(f"st32_{i}") for i in range(2)]
    st16_r = [Res(f"st16_{i}") for i in range(2)]
    arena0 = off[0]
    ARENA = 29696
    off[0] += ARENA
    assert off[0] <= nc.sbuf_top, (off[0], nc.sbuf_top)

    def arena_alloc():
        o = [arena0]

        def f(name, shape, dt):
            n = 1
            for d_ in shape[1:]:
                n *= d_
            nb = n * (4 if dt in (F32, I32) else 2)
            t_ = nc.alloc_sbuf_tensor_at(name, shape, dt, offset=o[0])
            o[0] += (nb + 63) // 64 * 64
            assert o[0] <= arena0 + ARENA, (name, o[0] - arena0)
            return t_
        return f
    ps = [nc.alloc_psum_tensor(f"ps{i}", [128, T], F32) for i in range(8)]
    ps_r = [Res(f"ps{i}", ps=True) for i in range(8)]
    C.xres, C.xr, C.ng, C.ng_r, C.ones_bf, C.ones_r = xres, xr, ng, ng_r, ones_bf, ones_r
    C.xn, C.xn_r, C.sq, C.sq_r, C.rs, C.rs_r = xn, xn_r, sq, sq_r, rs, rs_r
    C.ring, C.ps, C.ps_r = ring, ps, ps_r

    S.memset(ones_bf[:], 1.0, [ones_r])
    S.dma(ng[:], ngT, writes=[ng_r])
    S.dma(cmask[:], cmaskT, writes=[cmask_r])
    if "conv" in kinds:
        S.dma(ck[:], ckT, writes=[ck_r])
    xTv = xT.rearrange("(c p) n -> p c n", p=128)
    for t in range(NTL):
        S.dma(xres[:, :, t * T:(t + 1) * T], xTv[:, :, t * T:(t + 1) * T], writes=xr[t])

    stc = [0]

    def prefetch_tasks(g):
        tasks = []
        for r0 in range(0, g.rows, 128):
            nr = min(128, g.rows - r0)
            for c0 in range(0, SLOT, 512):
                def task(r0=r0, nr=nr, c0=c0):
                    i = stc[0] % 2
                    stc[0] += 1
                    S.dma(st32[i][0:nr, :], g.win[r0:r0 + nr, c0:c0 + 512], writes=[st32_r[i]])
                    S.copy(st16[i][0:nr, :], st32[i][0:nr, :], [st32_r[i]], [st16_r[i]], eng="pool")
                    S.dma(g.cin.ap()[r0:r0 + nr, c0:c0 + 512], st16[i][0:nr, :], reads=[st16_r[i]], writes=[g.cin_r])
                tasks.append(task)

        def coll():
            S.collective(lambda e: e.collective_compute(
                "AllGather", ALU.bypass, replica_groups=[list(range(NCORES))],
                ins=[g.cin.ap().opt()], outs=[g.cout.ap().opt()]), reads=[g.cin_r], writes=[g.cout_r])
        tasks.append(coll)
        return tasks
    def ring_load_n(src, src_r, _unused):
        return ring.load(src, src_r)
    bg = []

    def tick(n=1):
        for _ in range(n):
            if bg:
                bg.pop(0)()

    def drain_bg():
        while bg:
            bg.pop(0)()

    def norm_a(t, nbank):
        for c in range(KC):
            S.act(sq[c % 2][:], xres[:, c, t * T:(t + 1) * T], AF.Square, [xr[t][c]], [sq_r[c % 2]])
            S.mm(ps[nbank][:], ones_bf[:], sq[c % 2][:], c == 0, c == KC - 1, [sq_r[c % 2], ones_r], [ps_r[nbank]])

    def norm_b(t, nbank, gi, dst, dst_r):
        S.act(rs[:], ps[nbank][:], AF.Ln, [ps_r[nbank]], [rs_r], scale=1.0 / D, bias=EPS)
        S.act(rs[:], rs[:], AF.Exp, [rs_r], [rs_r], scale=-0.5)
        for c in range(KC):
            S.stt(dst[:, c, :], xres[:, c, t * T:(t + 1) * T], ng[:, gi * KC + c:gi * KC + c + 1], rs[:],
                  ALU.mult, ALU.mult, [xr[t][c], ng_r, rs_r], [dst_r])

    def ffn(g, gi, coef):
        al = arena_alloc()
        hbuf = al("hbuf", [128, FC, T], BF16)
        h_r = [Res(f"h{m}") for m in range(FC)]
        sg = [al(f"sg{i}", [128, T], F32) for i in range(2)]
        sg_r = [Res(f"sg{i}") for i in range(2)]
        norm_a(0, 7)
        norm_b(0, 7, gi, xn[0], xn_r[0])
        for t in range(NTL):
            cur = xn[t % 2]
            cur_r = xn_r[t % 2]
            for m in range(FC):
                w, w_r = ring.load(g.slab(m), g.cout_r)
                gb, ub = (2 * m) % 4, (2 * m + 1) % 4
                for half, bank in ((0, gb), (1, ub)):
                    for k in range(KC):
                        o = (half * KC + k) * 128
                        S.mm(ps[bank][:], w[:, o:o + 128], cur[:, k, :], k == 0, k == KC - 1, [w_r, cur_r], [ps_r[bank]])
                S.act(sg[m % 2][:], ps[gb][:], AF.Silu, [ps_r[gb]], [sg_r[m % 2]])
                S.tt(hbuf[:, m, :], sg[m % 2][:], ps[ub][:], ALU.mult, [sg_r[m % 2], ps_r[ub]], [h_r[m]])
                tick()
                if t + 1 < NTL:
                    if m == 4:
                        norm_a(t + 1, 7)
                    if m == 8:
                        norm_b(t + 1, 7, gi, xn[(t + 1) % 2], xn_r[(t + 1) % 2])
            corder = [4, 5, 6, 7, 0, 1, 2, 3]
            for kk in range(FC // 2):
                w, w_r = ring.load(g.slab(FC + kk), g.cout_r)
                for i in range(2):
                    k = 2 * kk + i
                    for c in corder:
                        o = i * D + c * 128
                        S.mm(ps[c][:], w[:, o:o + 128], hbuf[:, k, :], k == 0, k == FC - 1, [w_r, h_r[k]], [ps_r[c]])
                tick()
            for c in corder:
                xs = xres[:, c, t * T:(t + 1) * T]
                S.stt(xs, ps[c][:], float(coef), xs, ALU.mult, ALU.add, [ps_r[c], xr[t][c]], [xr[t][c]])


    def out_proj(g, slab0, y, y_r, t, coef):
        corder = [4, 5, 6, 7, 0, 1, 2, 3]
        for kk in range(KC // 2):
            w, w_r = ring.load(g.slab(slab0 + kk), g.cout_r)
            for i in range(2):
                k = 2 * kk + i
                for c in corder:
                    o = i * D + c * 128
                    S.mm(ps[c][:], w[:, o:o + 128], y[:, k, :], k == 0, k == KC - 1, [w_r] + y_r, [ps_r[c]])
            tick()
        for c in corder:
            xs = xres[:, c, t * T:(t + 1) * T]
            S.stt(xs, ps[c][:], float(coef), xs, ALU.mult, ALU.add, [ps_r[c], xr[t][c]], [xr[t][c]])

    def conv(g, gi, ci):
        al = arena_alloc()
        zbuf = al("zbuf", [128, KC, T + 2], F32)
        z_r = [Res(f"z{c}") for c in range(KC)]
        ybuf = al("ybuf", [128, KC, T], BF16)
        y_r = [Res(f"y{c}") for c in range(KC)]
        cg = al("cg", [128, T], F32)
        cg_r = Res("cg")
        acc = al("acc", [128, T], F32)
        acc_r = Res("acc")
        xm = al("xm", [128, KC, 2], BF16)
        xm_r = Res("xm")
        small = al("small", [128, 64], F32)
        small_r = Res("small")
        kb = ci * 24
        for c in range(KC):
            S.act(sq[0][:, 2 * c:2 * c + 2], xres[:, c, NT - 2:NT], AF.Square, [xr[NTL - 1][c]], [sq_r[0]])
        for c in range(KC):
            S.mm(ps[7][:, 0:2], ones_bf[:], sq[0][:, 2 * c:2 * c + 2], c == 0, c == KC - 1, [sq_r[0], ones_r], [ps_r[7]])
        S.act(small[:, 0:2], ps[7][:, 0:2], AF.Ln, [ps_r[7]], [small_r], scale=1.0 / D, bias=EPS)
        S.act(small[:, 0:2], small[:, 0:2], AF.Exp, [small_r], [small_r], scale=-0.5)
        for c in range(KC):
            S.stt(xm[:, c, :], xres[:, c, NT - 2:NT], ng[:, gi * KC + c:gi * KC + c + 1], small[:, 0:2],
                  ALU.mult, ALU.mult, [xr[NTL - 1][c], ng_r, small_r], [xm_r])
        for c in range(KC):
            w, w_r = ring.load(g.slab(c), g.cout_r)
            for half, bank in ((0, 5), (1, 6)):
                for k in range(KC):
                    o = (half * KC + k) * 128
                    S.mm(ps[bank][:, 0:2], w[:, o:o + 128], xm[:, k, :], k == 0, k == KC - 1, [w_r, xm_r], [ps_r[bank]])
            S.act(cg[:, 0:2], ps[5][:, 0:2], AF.Copy, [ps_r[5]], [cg_r])
            S.tt(small[:, 16 + 2 * c:18 + 2 * c], cg[:, 0:2], ps[6][:, 0:2], ALU.mult, [cg_r, ps_r[6]], [small_r])
        zin_r, zout_r = Res("zin"), Res("zout")
        S.dma(zc_in.ap(), small[:, 16:32], reads=[small_r], writes=[zin_r])
        S.collective(lambda e: e.collective_compute(
            "AllGather", ALU.bypass, replica_groups=[[0, 1], [2, 3], [4, 5], [6, 7]],
            ins=[zc_in.ap().opt()], outs=[zc_out.ap().opt()]), reads=[zin_r], writes=[zout_r])
        S.dma(small[:, 32:48], zc_out.ap()[0:128, :], reads=[zout_r], writes=[small_r])
        S.ts(zbuf[:, :, 0:2], small[:, 32:48].rearrange("p (c j) -> p c j", j=2), cmask[:, 1:2], None, ALU.mult, ALU.bypass,
             [small_r, cmask_r], z_r)
        norm_a(0, 7)
        norm_b(0, 7, gi, xn[0], xn_r[0])
        for t in range(NTL):
            cur, cur_r = xn[t % 2], xn_r[t % 2]
            if t > 0:
                S.copy(zbuf[:, :, 0:2], zbuf[:, :, T:T + 2], z_r, z_r)
            for c in range(KC):
                w, w_r = ring.load(g.slab(c), g.cout_r)
                for half, bank in ((0, 0), (1, 1)):
                    for k in range(KC):
                        o = (half * KC + k) * 128
                        S.mm(ps[bank][:], w[:, o:o + 128], cur[:, k, :], k == 0, k == KC - 1, [w_r, cur_r], [ps_r[bank]])
                if c % 2 == 0:
                    w2, w2_r = ring.load(g.slab(8 + c // 2), g.cout_r)
                for k in range(KC):
                    o = ((c % 2) * KC + k) * 128
                    S.mm(ps[2][:], w2[:, o:o + 128], cur[:, k, :], k == 0, k == KC - 1, [w2_r, cur_r], [ps_r[2]])
                S.act(cg[:], ps[0][:], AF.Copy, [ps_r[0]], [cg_r])
                S.tt(zbuf[:, c, 2:T + 2], cg[:], ps[1][:], ALU.mult, [cg_r, ps_r[1]], [z_r[c]])
                S.ts(acc[:], zbuf[:, c, 2:T + 2], ck[:, kb + 16 + c:kb + 17 + c], None, ALU.mult, ALU.bypass, [z_r[c], ck_r], [acc_r])
                S.stt(acc[:], zbuf[:, c, 1:T + 1], ck[:, kb + 8 + c:kb + 9 + c], acc[:], ALU.mult, ALU.add, [z_r[c], ck_r, acc_r], [acc_r])
                S.stt(acc[:], zbuf[:, c, 0:T], ck[:, kb + c:kb + 1 + c], acc[:], ALU.mult, ALU.add, [z_r[c], ck_r, acc_r], [acc_r])
                S.tt(ybuf[:, c, :], acc[:], ps[2][:], ALU.mult, [acc_r, ps_r[2]], [y_r[c]])
                tick()
                if t + 1 < NTL:
                    if c == 2:
                        norm_a(t + 1, 7)
                    if c == 4:
                        norm_b(t + 1, 7, gi, xn[(t + 1) % 2], xn_r[(t + 1) % 2])
            out_proj(g, 12, ybuf, y_r, t, 1.0)


    def mla(g, gi):
        SQ = 2 * NT
        NS = SQ // T
        scale = 96.0 ** -0.5
        barrier = S.barrier
        al = arena_alloc()
        stg = al("stg", [128, 3072], F32)
        stg_r = Res("stg")
        posi = al("posi", [32, T], I32)
        posf = al("posf", [32, T], F32)
        kf = al("kf", [32, T], F32)
        ki = al("ki", [32, T], I32)
        rr = al("rr", [32, T], F32)
        cst = al("cst", [32, 2, T], F32)
        cst16 = al("cst16", [32, 2, T], BF16)
        tr = Res("ropetmp")
        S.dma(mg[:], mlag, writes=[mconst_r])
        S.dma(rpc[:], ropec, writes=[mconst_r])
        S.dma(stg[:, 0:320], mlac, writes=[stg_r])
        S.copy(mc16[:], stg[:, 0:256], [stg_r], [mconst_r])
        S.copy(sel65[:], stg[:, 256:320], [stg_r], [mconst_r])
        barrier()
        cs_r = Res("cs_tab")
        TWO_PI = 2.0 * np.pi
        c1 = float(np.float32(6.28125))
        c2 = float(np.float32(TWO_PI - 6.28125))
        c3 = float(TWO_PI - 6.28125 - np.float64(np.float32(TWO_PI - 6.28125)))
        for st in range(NS):
            S.dma(posi[:], posT[:, st * T:(st + 1) * T], writes=[tr])
            S.copy(posf[:], posi[:], [tr], [tr])
            S.ts(posf[:], posf[:], rpc[:, 0:1], None, ALU.mult, ALU.bypass, [tr, mconst_r], [tr])
            S.ts(kf[:], posf[:], float(1.0 / TWO_PI), None, ALU.mult, ALU.bypass, [tr], [tr])
            S.copy(ki[:], kf[:], [tr], [tr])
            S.copy(kf[:], ki[:], [tr], [tr])
            S.stt(rr[:], kf[:], -c1, posf[:], ALU.mult, ALU.add, [tr], [tr])
            S.stt(rr[:], kf[:], -c2, rr[:], ALU.mult, ALU.add, [tr], [tr])
            S.stt(rr[:], kf[:], -c3, rr[:], ALU.mult, ALU.add, [tr], [tr])
            for ti, shift in ((0, float(np.pi / 2)), (1, 0.0)):
                y = cst[:, ti, :]
                S.ts(y, rr[:], shift, None, ALU.add, ALU.bypass, [tr], [tr])
                S.ts(kf[:], y, float(np.pi), None, ALU.is_gt, ALU.bypass, [tr], [tr])
                S.stt(y, kf[:], -TWO_PI, y, ALU.mult, ALU.add, [tr], [tr])
                S.ts(kf[:], y, -float(np.pi), None, ALU.is_lt, ALU.bypass, [tr], [tr])
                S.stt(y, kf[:], TWO_PI, y, ALU.mult, ALU.add, [tr], [tr])
            S.act(cst[:, 0, :], cst[:, 0, :], AF.Sin, [tr], [tr])
            S.act(cst[:, 1, :], cst[:, 1, :], AF.Sin, [tr], [tr])
            S.ts(cst[:, 1, :], cst[:, 1, :], rpc[:, 1:2], None, ALU.mult, ALU.bypass, [tr, mconst_r], [tr])
            S.copy(cst16[:], cst[:], [tr], [tr])
            S.dma(cs_tab.ap()[0:32, st * T:(st + 1) * T], cst16[:, 0, :], reads=[tr], writes=[cs_r])
            S.dma(cs_tab.ap()[32:64, st * T:(st + 1) * T], cst16[:, 1, :], reads=[tr], writes=[cs_r])
        barrier()

        al = arena_alloc()
        lq = al("lq", [128, 2, T], BF16)
        lkv = al("lkv", [128, T], BF16)
        lpe = al("lpe", [32, T], BF16)
        sqa = al("sqa", [128, T], BF16)
        sqb_ = al("sqb", [128, T], BF16)
        r1 = al("r1", [128, T], F32)
        m1r = Res("m1")
        lat_in_r = [Res(f"lat_in{j}") for j in range(NLC)]
        lat_out_r = [Res(f"lat_out{j}") for j in range(NLC)]
        TPC = LC // T
        norm_a(0, 7)
        norm_b(0, 7, gi, xn[0], xn_r[0])
        for t in range(NTL):
            cur, cur_r = xn[t % 2], xn_r[t % 2]
            ws = [ring_load_n(g.slab(i), g.cout_r, i + 2) for i in range(2)]
            for (c0, m, bank) in ((0, 128, 0), (128, 128, 1), (256, 128, 2), (384, 32, 3)):
                for k in range(KC):
                    w, w_r = ws[k // 4]
                    o = (k % 4) * 416 + c0
                    S.mm(ps[bank][0:m, :], w[:, o:o + m], cur[:, k, :], k == 0, k == KC - 1, [w_r, cur_r], [ps_r[bank]])
            S.act(sqa[:], ps[0][:], AF.Square, [ps_r[0]], [m1r])
            S.act(sqb_[:], ps[1][:], AF.Square, [ps_r[1]], [m1r])
            S.mm(ps[4][:], ones_bf[:], sqa[:], True, False, [m1r, ones_r], [ps_r[4]])
            S.mm(ps[4][:], ones_bf[:], sqb_[:], False, True, [m1r, ones_r], [ps_r[4]])
            S.act(r1[:], ps[4][:], AF.Ln, [ps_r[4]], [m1r], scale=1.0 / 256, bias=EPS)
            S.act(r1[:], r1[:], AF.Exp, [m1r], [m1r], scale=-0.5)
            for j in range(2):
                S.stt(lq[:, j, :], ps[j][:], mg[:, 4 + j:5 + j], r1[:], ALU.mult, ALU.mult, [ps_r[j], m1r, mconst_r], [m1r])
            lj, lc0 = t // TPC, (t % TPC) * T
            S.dma(lat_in[lj].ap()[0:256, lc0:lc0 + T].rearrange("(j p) n -> p j n", p=128), lq[:], reads=[m1r], writes=[lat_in_r[lj]])
            S.act(sqa[:], ps[2][:], AF.Square, [ps_r[2], m1r], [m1r])
            S.mm(ps[5][:], ones_bf[:], sqa[:], True, True, [m1r, ones_r], [ps_r[5]])
            S.act(r1[:], ps[5][:], AF.Ln, [ps_r[5], m1r], [m1r], scale=1.0 / 128, bias=EPS)
            S.act(r1[:], r1[:], AF.Exp, [m1r], [m1r], scale=-0.5)
            S.stt(lkv[:], ps[2][:], mg[:, 6:7], r1[:], ALU.mult, ALU.mult, [ps_r[2], m1r, mconst_r], [m1r])
            S.dma(lat_in[lj].ap()[256:384, lc0:lc0 + T], lkv[:], reads=[m1r], writes=[lat_in_r[lj]])
            S.copy(lpe[:], ps[3][0:32, :], [ps_r[3], m1r], [m1r])
            S.dma(lat_in[lj].ap()[384:416, lc0:lc0 + T], lpe[:], reads=[m1r], writes=[lat_in_r[lj]])
            if (t + 1) % TPC == 0:
                S.collective(lambda e, lj=lj: e.collective_compute(
                    "AllGather", ALU.bypass, replica_groups=[[0, 1], [2, 3], [4, 5], [6, 7]],
                    ins=[lat_in[lj].ap().opt()], outs=[lat_out[lj].ap().opt()]), reads=[lat_in_r[lj]], writes=[lat_out_r[lj]])
            tick(4)
            if t + 1 < NTL:
                norm_a(t + 1, 7)
                norm_b(t + 1, 7, gi, xn[(t + 1) % 2], xn_r[(t + 1) % 2])
        barrier()

        al = arena_alloc()
        stg = al("stg2", [128, 3072], F32)
        S.dma(stg[:], mlahw, writes=[stg_r])
        hq, hq_r = ring.t[0], ring.r[0]
        hkv, hkv_r = ring.t[1], ring.r[1]
        S.copy(hq[:], stg[:, 0:2048], [stg_r], [hq_r])
        S.copy(hkv[:, 0:1024], stg[:, 2048:3072], [stg_r], [hkv_r])
        barrier()
        al = arena_alloc()
        KT = al("KT", [128, SQ], BF16)
        Vh = al("Vh", [128, SQ // 128, 65], BF16)
        Oa = al("Oa", [128, T], F32)
        rl = al("rl", [128, T], F32)
        xo = [xn_off]

        def xa(name, shape, dt):
            n = 1
            for d_ in shape[1:]:
                n *= d_
            nb = n * (4 if dt in (F32, I32) else 2)
            t_ = nc.alloc_sbuf_tensor_at(name, shape, dt, offset=xo[0])
            xo[0] += nb
            assert xo[0] <= xn_off + 16384
            return t_
        pt = [xa(f"pt{i}", [128, T], BF16) for i in range(4)]
        qts = [xa(f"qts{i}", [128, T], BF16) for i in range(2)]
        cqt = [xa(f"cqt{i}", [128, 2, T], BF16) for i in range(2)]
        CS = [xa(f"CS{i}", [128, 2, T], BF16) for i in range(2)]
        ro = [ring_off + 2 * 4096]

        def ra(name, shape, dt):
            n = 1
            for d_ in shape[1:]:
                n *= d_
            nb = n * (4 if dt in (F32, I32) else 2)
            t_ = nc.alloc_sbuf_tensor_at(name, shape, dt, offset=ro[0])
            ro[0] += nb
            assert ro[0] <= ring_off + 5 * 4096
            return t_
        Rb = [ra(f"Rb{i}", [128, T], F32) for i in range(2)]
        tA = [ra(f"tA{i}", [128, T], BF16) for i in range(2)]
        tB = [ra(f"tB{i}", [128, T], BF16) for i in range(2)]
        sqh = [ra(f"sqh{i}", [128, T], BF16) for i in range(2)]
        on = sq[0]
        kt_r, vh_r, pt_r = Res("KT"), Res("Vh"), [Res("pt0"), Res("pt1"), Res("pt2"), Res("pt3")]
        qts_r = [Res("qts0"), Res("qts1")]
        in_r = [Res("in0"), Res("in1")]
        cs_in_r = [Res("cs0"), Res("cs1")]
        sqh_r = [Res("sqh0"), Res("sqh1")]
        R_r = [Res("R0"), Res("R1")]
        tab_r = [Res("tab0"), Res("tab1")]
        w_r_ = Res("work")
        oin_r = [Res(f"o_in{h}") for h in range(8)]
        oout_r = [Res(f"o_out{h}") for h in range(8)]
        onr = Res("on")
        S.memset(Vh[:, :, 64:65], 1.0, [vh_r])
        selI = mc16[0:32, 0:96]
        selSw = mc16[0:32, 96:128]
        m01 = mc16[:, 128:256]

        def lat_cols(st):
            r_, lt = st // NTL, st % NTL
            return r_ * 416, (lt % TPC) * T, lt // TPC

        def prep(kind, st, b, hl, dst, dst_r):
            A = b if kind == "k" else 0
            Bk = 6 + b if kind == "k" else 6
            ON = 4 + b if kind == "k" else 7
            gcol = 0 if kind == "q" else 2
            r0, c0, lj = lat_cols(st)
            Ct, St = CS[b][:, 0, :], CS[b][:, 1, :]
            if kind == "q":
                S.dma(cqt[b][:], lat_out[lj].ap()[r0:r0 + 256, c0:c0 + T].rearrange("(j p) n -> p j n", p=128),
                      reads=[lat_out_r[lj]], writes=[in_r[b]])
            else:
                S.dma(cqt[b][:, 0, :], lat_out[lj].ap()[r0 + 256:r0 + 384, c0:c0 + T], reads=[lat_out_r[lj]], writes=[in_r[b]])
                S.dma(cqt[b][0:32, 1, :], lat_out[lj].ap()[r0 + 384:r0 + 416, c0:c0 + T], reads=[lat_out_r[lj]], writes=[in_r[b]])
            S.dma(Ct[64:96, :], cs_tab.ap()[0:32, st * T:(st + 1) * T], reads=[cs_r], writes=[cs_in_r[b]])
            S.dma(St[64:96, :], cs_tab.ap()[32:64, st * T:(st + 1) * T], reads=[cs_r], writes=[cs_in_r[b]])
            yield
            if kind == "q":
                wq = hq[:, hl * 256:(hl + 1) * 256]
                for kc in range(2):
                    S.mm(ps[A][0:96, :], wq[:, kc * 128:kc * 128 + 96], cqt[b][:, kc, :], kc == 0, kc == 1, [hq_r, in_r[b]], [ps_r[A]])
                for kc in range(2):
                    S.mm(ps[Bk][64:96, :], wq[:, kc * 128 + 96:kc * 128 + 128], cqt[b][:, kc, :], kc == 0, kc == 1, [hq_r, in_r[b]], [ps_r[Bk]])
            else:
                wk = hkv[:, hl * 128:hl * 128 + 64]
                wv = hkv[:, hl * 128 + 64:hl * 128 + 128]
                ckvt, kpet = cqt[b][:, 0, :], cqt[b][0:32, 1, :]
                S.mm(ps[A][0:64, :], wk, ckvt, True, True, [hkv_r, in_r[b]], [ps_r[A]])
                S.mm(ps[A][64:96, :], selI[:, 64:96], kpet, True, True, [mconst_r, in_r[b]], [ps_r[A]])
                S.mm(ps[Bk][64:96, :], selSw, kpet, True, True, [mconst_r, in_r[b]], [ps_r[Bk]])
                for jb in range(4):
                    S.mm(ps[2 + b][:, jb * 64:(jb + 1) * 64], cqt[b][:, 0, jb * 128:(jb + 1) * 128], wv, True, True, [in_r[b], hkv_r], [ps_r[2 + b]])
            yield
            S.act(sqh[b][0:96, :], ps[A][0:96, :], AF.Square, [ps_r[A]], [sqh_r[b]])
            if kind == "k":
                S.act(Vh[:, st * 4:(st + 1) * 4, 0:64], ps[2 + b][:, 0:256].rearrange("p (j d) -> p j d", d=64), AF.Copy, [ps_r[2 + b]], [vh_r])
            yield
            S.mm(ps[ON][0:96, :], ones_bf[0:96, 0:96], sqh[b][0:96, :], True, True, [sqh_r[b], ones_r], [ps_r[ON]])
            yield
            R = Rb[b]
            S.act(R[0:96, :], ps[ON][0:96, :], AF.Ln, [ps_r[ON]], [R_r[b]], scale=1.0 / 96, bias=EPS)
            S.act(R[0:96, :], R[0:96, :], AF.Exp, [R_r[b]], [R_r[b]], scale=-0.5)
            yield
            S.stt(dst[0:64, :], ps[A][0:64, :], mg[0:64, gcol:gcol + 1], R[0:64, :], ALU.mult, ALU.mult,
                  [ps_r[A], R_r[b], mconst_r], [dst_r])
            S.stt(tA[b][64:96, :], ps[A][64:96, :], mg[64:96, gcol:gcol + 1], Ct[64:96, :], ALU.mult, ALU.mult,
                  [ps_r[A], cs_in_r[b], mconst_r], [tab_r[b]])
            S.stt(tB[b][64:96, :], ps[Bk][64:96, :], mg[64:96, gcol + 1:gcol + 2], St[64:96, :], ALU.mult, ALU.mult,
                  [ps_r[Bk], cs_in_r[b], mconst_r], [tab_r[b]])
            S.tt(tA[b][64:96, :], tA[b][64:96, :], tB[b][64:96, :], ALU.add, [tab_r[b]], [tab_r[b]])
            S.tt(dst[64:96, :], tA[b][64:96, :], R[64:96, :], ALU.mult, [tab_r[b], R_r[b]], [dst_r])

        def run_all(g_):
            for _ in g_:
                pass

        for hl in range(8):
            gens = []
            nxt = 0
            while gens or nxt < NS:
                if len(gens) < KFLIGHT and nxt < NS:
                    gens.append(prep("k", nxt, nxt % 2, hl, KT[:, nxt * T:(nxt + 1) * T], kt_r))
                    nxt += 1
                    tick()
                for g_ in list(gens):
                    try:
                        next(g_)
                    except StopIteration:
                        gens.remove(g_)
            run_all(prep("q", 0, 0, hl, qts[0], qts_r[0]))
            for qt in range(NS):
                qb = qt % 2
                nxtg = prep("q", qt + 1, (qt + 1) % 2, hl, qts[(qt + 1) % 2], qts_r[(qt + 1) % 2]) if qt + 1 < NS else iter(())
                nkb = qt * 4 + 4

                def s_step(kb, qt=qt, qb=qb):
                    j = kb - qt * 4
                    cs0 = 0 if j <= 0 else j * 128
                    sb_ = 1 + kb % 3
                    S.mm(ps[sb_][:, cs0:T], KT[0:96, kb * 128:(kb + 1) * 128], qts[qb][0:96, cs0:T], True, True, [kt_r, qts_r[qb]], [ps_r[sb_]])
                    p_, p_r = pt[kb % 4], pt_r[kb % 4]
                    S.act(p_[:, cs0:T], ps[sb_][:, cs0:T], AF.Exp, [ps_r[sb_]], [p_r], scale=scale)
                    if j >= 0:
                        S.tt(p_[:, cs0:cs0 + 128], p_[:, cs0:cs0 + 128], m01, ALU.mult, [p_r, mconst_r], [p_r], eng="pool")

                def pv_step(kb, qt=qt, nkb=nkb):
                    j = kb - qt * 4
                    cs0 = 0 if j <= 0 else j * 128
                    S.mm(ps[4][0:65, cs0:T], Vh[:, kb, :], pt[kb % 4][:, cs0:T], kb == 0, kb == nkb - 1, [vh_r, pt_r[kb % 4]], [ps_r[4]])
                s_step(0)
                s_step(1)
                for kb in range(nkb):
                    if kb + 2 < nkb:
                        s_step(kb + 2)
                    pv_step(kb)
                    if QINTER:
                        next(nxtg, None)
                run_all(nxtg)
                S.copy(Oa[0:65, :], ps[4][0:65, :], [ps_r[4]], [w_r_])
                S.mm(ps[5][0:64, :], sel65[0:65, :], Oa[0:65, :], True, True, [w_r_, mconst_r], [ps_r[5]])
                S.op("dve", lambda e: e.reciprocal(rl[0:64, :], ps[5][0:64, :]), [ps_r[5]], [w_r_])
                S.tt(on[0:64, :], Oa[0:64, :], rl[0:64, :], ALU.mult, [w_r_, onr], [onr])
                S.dma(o_in[hl].ap()[:, qt * T:(qt + 1) * T], on[0:64, :], reads=[onr], writes=[oin_r[hl]])
                tick()
            S.collective(lambda e, hl=hl: e.collective_compute(
                "AllGather", ALU.bypass, replica_groups=[[0, 1], [2, 3], [4, 5], [6, 7]],
                ins=[o_in[hl].ap().opt()], outs=[o_out[hl].ap().opt()]), reads=[oin_r[hl]], writes=[oout_r[hl]])
        barrier()

        al = arena_alloc()
        cand = [al(f"cand{i}", [128, KC, T], BF16) for i in range(2)]
        yb = al("yb", [128, KC, T], BF16)
        tmp = al("tmp3", [128, T], F32)
        cand_r, y_r3, tmp_r = Res("cand"), [Res(f"y3_{c}") for c in range(KC)], Res("tmp3")
        for t in range(NTL):
            for i in range(2):
                for gh in range(16):
                    r_, hl_ = gh // 8, gh % 8
                    S.dma(cand[i][(gh % 2) * 64:(gh % 2) * 64 + 64, gh // 2, :],
                          o_out[hl_].ap()[r_ * 64:(r_ + 1) * 64, i * NT + t * T:i * NT + (t + 1) * T],
                          reads=[oout_r[hl_]], writes=[cand_r])
            for c in range(KC):
                S.ts(tmp[:], cand[0][:, c, :], cmask[:, 0:1], None, ALU.mult, ALU.bypass, [cand_r, cmask_r], [tmp_r])
                S.stt(yb[:, c, :], cand[1][:, c, :], cmask[:, 1:2], tmp[:], ALU.mult, ALU.add, [cand_r, cmask_r, tmp_r], [y_r3[c]])
            out_proj(g, 2, yb, y_r3, t, 1.0)
        barrier()

    def sgm(g, gi):
        al = arena_alloc()
        ub = al("ub", [128, KC, T], BF16)
        u_r = [Res(f"u{c}") for c in range(KC)]
        vn = al("vn", [128, 4, D], BF16)
        vn_r = [Res(f"vn{j}") for j in range(4)]
        vg = al("vg", [128, D], F32)
        vg_r = Res("vg")
        wm = al("wm", [128, D], BF16)
        wm_r = Res("wm")
        gvn = al("gvn", [128, D], F32)
        gvn_r = Res("gvn")
        bsb = al("bsb", [128, D], BF16)
        bsb_r = Res("bsb")
        S.dma(gvn[:], sgvnT, writes=[gvn_r])
        S.dma(vg[:], wsT, writes=[vg_r])
        S.dma(rs[:, 0:128], mask01, writes=[rs_r])
        for gg in range(8):
            S.tt(wm[:, gg * 128:(gg + 1) * 128], vg[:, gg * 128:(gg + 1) * 128], rs[:, 0:128], ALU.mult, [vg_r, rs_r], [wm_r])
        S.barrier()
        S.dma(vg[0:1, :], sgbT, writes=[vg_r])
        S.memset(bsb[:], 0.0, [bsb_r])
        S.copy(bsb[0:1, :], vg[0:1, :], [vg_r, bsb_r], [bsb_r])
        S.barrier()
        norm_a(0, 7)
        norm_b(0, 7, gi, xn[0], xn_r[0])
        for t in range(NTL):
            cur, cur_r = xn[t % 2], xn_r[t % 2]
            for cc in range(4):
                w, w_r = ring.load(g.slab(cc), g.cout_r)
                for j in range(2):
                    c = 2 * cc + j
                    for k in range(KC):
                        o = (j * KC + k) * 128
                        S.mm(ps[c % 4][:], w[:, o:o + 128], cur[:, k, :], k == 0, k == KC - 1, [w_r, cur_r], [ps_r[c % 4]])
                    S.act(ub[:, c, :], ps[c % 4][:], AF.Gelu, [ps_r[c % 4]], [u_r[c]])
                tick()
            for kk in range(4):
                w, w_r = ring.load(g.slab(4 + kk), g.cout_r)
                for i in range(2):
                    k = 2 * kk + i
                    for j in range(4):
                        for hf in range(2):
                            S.mm(ps[j * 2 + hf][:], cur[:, k, j * 128:(j + 1) * 128], w[:, i * D + hf * 512:i * D + hf * 512 + 512],
                                 k == 0, k == KC - 1, [w_r, cur_r], [ps_r[j * 2 + hf]])
                tick()
            for j in range(4):
                for hf in range(2):
                    S.act(vg[:, hf * 512:(hf + 1) * 512], ps[j * 2 + hf][:], AF.Gelu, [ps_r[j * 2 + hf]], [vg_r])
                S.op("act", lambda e, j=j: e.activation(out=vn[:, j, :], in_=vg[:], func=AF.Square, accum_out=sg_ss[:, j:j + 1]),
                     [vg_r], [vn_r[j], ss_r])
                S.act(sg_ss[:, 4 + j:5 + j], sg_ss[:, j:j + 1], AF.Ln, [ss_r], [ss_r], scale=1.0 / D, bias=EPS)
                S.act(sg_ss[:, 4 + j:5 + j], sg_ss[:, 4 + j:5 + j], AF.Exp, [ss_r], [ss_r], scale=-0.5)
                S.stt(vn[:, j, :], vg[:], sg_ss[:, 4 + j:5 + j], gvn[:], ALU.mult, ALU.mult, [vg_r, ss_r, gvn_r], [vn_r[j]])
            if t + 1 < NTL:
                norm_a(t + 1, 7)
                norm_b(t + 1, 7, gi, xn[(t + 1) % 2], xn_r[(t + 1) % 2])
            for gg in range(8):
                b = gg % 6
                for j in range(4):
                    S.mm(ps[b][:, j * 128:(j + 1) * 128], vn[:, j, gg * 128:(gg + 1) * 128], wm[:, gg * 128:(gg + 1) * 128],
                         True, False, [vn_r[j], wm_r], [ps_r[b]])
                    S.mm(ps[b][:, j * 128:(j + 1) * 128], ones_bf[:], bsb[:, gg * 128:(gg + 1) * 128],
                         False, True, [ones_r, bsb_r], [ps_r[b]])
                S.tt(ub[:, gg, :], ub[:, gg, :], ps[b][:], ALU.mult, [u_r[gg], ps_r[b]], [u_r[gg]])
            out_proj(g, 8, ub, u_r, t, 1.0)

    bg.extend(prefetch_tasks(groups[0]))
    drain_bg()
    for li, lay in enumerate(layers):
        kind, idx = lay[0], lay[1]
        if li + 1 < len(layers):
            bg.extend(prefetch_tasks(groups[li + 1]))
        if kind == "ffn":
            ffn(groups[li], idx, 0.5)
        elif kind == "conv":
            conv(groups[li], idx, layers[li][2])
        elif kind == "sg":
            sgm(groups[li], idx)
        elif kind == "mla":
            mla(groups[li], idx)
        drain_bg()
        S.barrier()

    oTv = outT.rearrange("(c p) n -> p c n", p=128)
    outs = []
    for t in range(NTL):
        outs.append(S.dma(oTv[:, :, t * T:(t + 1) * T], xres[:, :, t * T:(t + 1) * T], reads=xr[t]))
    S.finish_wait("sp", outs)
    S.emit()
    return nc


def ffn_tiles(g, u, d):
    return np.concatenate([tile_pair(g, u), tile_rows(d, 2)], axis=0).reshape(33 * 128, SLOT)


def conv_tiles(w_in, w_out):
    wb, wc, wh = w_in[:, :D], w_in[:, D:2 * D], w_in[:, 2 * D:]
    return np.concatenate([tile_pair(wc, wh), tile_single(wb).reshape(4, 2, 128, D).transpose(0, 2, 1, 3).reshape(4, 128, SLOT),
                           tile_rows(w_out, 2)], axis=0).reshape(16 * 128, SLOT)


def sg_tiles(w_in, w_out):
    wu, wv = w_in[:, :D], w_in[:, D:]
    return np.concatenate([tile_single(wu).reshape(4, 2, 128, D).transpose(0, 2, 1, 3).reshape(4, 128, SLOT),
                           tile_rows(wv, 2), tile_rows(w_out, 2)], axis=0).reshape(12 * 128, SLOT)


def mla_tiles(w_a, w_o):
    wa = np.zeros((2, 128, SLOT), np.float32)
    t_ = w_a.reshape(8, 128, 416)
    for s_ in range(2):
        wa[s_, :, :4 * 416] = t_[4 * s_:4 * s_ + 4].transpose(1, 0, 2).reshape(128, 4 * 416)
    pad = np.zeros((2, 128, SLOT), np.float32)
    return np.concatenate([wa, tile_rows(w_o, 2), pad], axis=0).reshape(8 * 128, SLOT)


def mla_head_weights(w_uq, w_ukv, half):
    hq = np.zeros((128, 8, 2, 128), np.float32)
    hkv = np.zeros((128, 8, 128), np.float32)
    sw = [64 + (m + 16) % 32 for m in range(32)]
    for hl in range(8):
        h = half * 8 + hl
        wq = w_uq[:, h * 96:(h + 1) * 96]
        full = np.concatenate([wq, wq[:, sw]], axis=1)
        hq[:, hl] = full.reshape(2, 128, 128).transpose(1, 0, 2)
        hkv[:, hl] = w_ukv[:, h * 128:(h + 1) * 128]
    return np.ascontiguousarray(np.concatenate([hq.reshape(128, 2048), hkv.reshape(128, 1024)], axis=1))


def mla_consts(q_norm, kv_norm, q_gain, k_gain):
    sw = [64 + (m + 16) % 32 for m in range(32)]
    mg = np.zeros((128, 8), np.float32)
    mg[:96, 0] = q_gain
    mg[64:96, 1] = q_gain[sw]
    mg[:96, 2] = k_gain
    mg[64:96, 3] = k_gain[sw]
    mg[:, 4] = q_norm[:128]
    mg[:, 5] = q_norm[128:]
    mg[:, 6] = kv_norm
    mc = np.zeros((128, 320), np.float32)
    for m in range(32):
        mc[m, 64 + m] = 1.0
        mc[(m + 16) % 32, 96 + m] = 1.0
    mc[:, 128:256] = np.triu(np.ones((128, 128), np.float32))
    mc[64, 256:320] = 1.0
    inv_freq = (1.0 / (np.float32(10000.0) ** (np.arange(0, 32, 2, dtype=np.float32) / np.float32(32)))).astype(np.float32)
    rc = np.zeros((32, 2), np.float32)
    rc[:, 0] = np.concatenate([inv_freq, inv_freq])
    rc[:16, 1] = -1.0
    rc[16:, 1] = 1.0
    return mg, mc, rc


def shard_rows(w, c):
    r = w.shape[0] // NCORES
    return np.ascontiguousarray(w[c * r:(c + 1) * r])


LAYERS = [("ffn", 0), ("conv", 1, 0), ("ffn", 2),
          ("ffn", 3), ("mla", 4), ("ffn", 5),
          ("ffn", 6), ("sg", 7), ("ffn", 8),
          ("ffn", 9), ("conv", 10, 1), ("ffn", 11)]


def kernel(**inputs):
    f32 = lambda a: np.ascontiguousarray(np.asarray(a, dtype=np.float32))
    x = f32(inputs["x"])
    B, SEQ, _ = x.shape
    NT = SEQ // 2
    pos = np.ascontiguousarray(np.asarray(inputs["positions"]).astype(np.int32))
    ng = f32(inputs["norm_g"]).reshape(12, D)
    ngT = np.ascontiguousarray(ng.reshape(12, KC, 128).transpose(2, 0, 1).reshape(128, 12 * KC))
    fg, fu, fd = f32(inputs["ffn_gate"]), f32(inputs["ffn_up"]), f32(inputs["ffn_down"])
    cw_in, cw_k, cw_out = f32(inputs["conv_w_in"]), f32(inputs["conv_k"]), f32(inputs["conv_w_out"])
    tiles = []
    for lay in LAYERS:
        if lay[0] == "ffn":
            L, j = lay[1] // 3, (0 if lay[1] % 3 == 0 else 1)
            tiles.append(ffn_tiles(fg[L, j], fu[L, j], fd[L, j]))
        elif lay[0] == "conv":
            tiles.append(conv_tiles(cw_in[lay[2]], cw_out[lay[2]]))
        elif lay[0] == "mla":
            tiles.append(mla_tiles(f32(inputs["mla_w_a"])[0], f32(inputs["mla_w_o"])[0]))
        else:
            tiles.append(sg_tiles(f32(inputs["sg_w_in"])[0], f32(inputs["sg_w_out"])[0]))
    ckT = np.zeros((128, 48), np.float32)
    for l in range(2):
        ckT[:, l * 24:(l + 1) * 24] = cw_k[l].reshape(3, KC, 128).transpose(2, 0, 1).reshape(128, 24)
    mg, mc, rc = mla_consts(f32(inputs["mla_q_norm"])[0], f32(inputs["mla_kv_norm"])[0],
                            f32(inputs["mla_q_gain"])[0], f32(inputs["mla_k_gain"])[0])
    w_uq, w_ukv = f32(inputs["mla_w_uq"])[0], f32(inputs["mla_w_ukv"])[0]
    sgvn = np.ascontiguousarray(np.broadcast_to(f32(inputs["sg_v_norm"])[0][None], (128, D)))
    wsT = np.ascontiguousarray(f32(inputs["sg_w_s"])[0].transpose(2, 0, 1).reshape(128, D))
    mask01 = np.triu(np.ones((128, 128), np.float32))
    sgb = f32(inputs["sg_b"])[0].reshape(1, D)
    in_maps = []
    for c in range(NCORES):
        b, half = c // 2, c % 2
        m = dict(xT=np.ascontiguousarray(x[b, half * NT:(half + 1) * NT].T), ngT=ngT,
                 cmask=np.tile(np.array([[1 - half, half]], np.float32), (128, 1)), ckT=ckT,
                 pos32=np.ascontiguousarray(np.broadcast_to(pos[b][None], (32, SEQ))), ropec=rc, mlag=mg, mlac=mc,
                 mlahw=mla_head_weights(w_uq, w_ukv, half), sgvn=sgvn, wsT=wsT, mask01=mask01, sgb=sgb)
        for i, t_ in enumerate(tiles):
            m[f"w_g{i}"] = shard_rows(t_, c)
        in_maps.append(m)
    nc = build(NT, LAYERS)
    res = run_bass_kernel_spmd(nc, in_maps, core_ids=list(range(NCORES)))
    out = np.empty_like(x)
    for c in range(NCORES):
        b, half = c // 2, c % 2
        out[b, half * NT:(half + 1) * NT] = res.results[c]["outT"].T
    return out
```
